# Optimizing a Trainium2 kernel written in Bass

```python
import math
import jax, jax.numpy as jnp
from jax import lax
import numpy as np


D_MODEL = 1024
BATCH = 8
SEQ = 2048
DEPTH = 1
DEC_BATCH = 2
DEC_SEQ = 8192
PAST_LEN = 128

D_SSM = 512
SSM_GROUP = 16
N_SSM_GROUPS = D_SSM // SSM_GROUP
STATE_P = 64
N_HEADS = 8
HEAD_DIM = 64
D_ATTN = N_HEADS * HEAD_DIM
D_MIX = D_SSM + D_ATTN
SPLITS = (D_SSM, D_SSM, D_ATTN, D_ATTN, D_ATTN, D_ATTN)
D_IN_PROJ = sum(SPLITS)
GRID_W = 64
WIN_H = 8
WIN_W = 16
Q_BLOCK_W = 16
K_BLOCK_W = 32
N_COL_BLOCKS = GRID_W // Q_BLOCK_W
DT_MIN = 1e-3
DT_MAX = 1e-1
EPS = 1e-6

kernel_name = 'hymba_s5_natten_bidir_encoder'


def rmsnorm(x, g):
    x32 = x.astype(jnp.float32)
    y = x32 * lax.rsqrt(jnp.mean(x32 * x32, axis=-1, keepdims=True) + EPS)
    return (y * g.astype(jnp.float32)).astype(x.dtype)


def _linear_scan(a, b):
    def combine(left, right):
        a1, b1 = left
        a2, b2 = right
        return a1 * a2, a2 * b1 + b2
    _, h = lax.associative_scan(combine, (a, b), axis=1)
    return h


def s5_direction(u_c, lam_re, lam_im, b_re, b_im, c_re, c_im, log_dt):
    f32 = jnp.float32
    lam = lax.complex(lam_re.astype(f32), lam_im.astype(f32))
    dt = jnp.exp(log_dt.astype(f32))[:, None]
    a_bar = jnp.exp(lam * dt)
    b = lax.complex(b_re.astype(f32), b_im.astype(f32))
    b_bar = ((a_bar - 1.0) / lam)[..., None] * b
    bu = jnp.einsum('gpc,blgc->blgp', b_bar, u_c)
    h = _linear_scan(jnp.broadcast_to(a_bar, bu.shape), bu)
    c = lax.complex(c_re.astype(f32), c_im.astype(f32))
    return jnp.einsum('gcp,blgp->blgc', c, h).real


def s5_mixer(u, lam_re, lam_im, b_re, b_im, c_re, c_im, log_dt, d_skip, w_glu, b_glu):
    bt, L, _ = u.shape
    u32 = u.astype(jnp.float32)
    u_c = u32.reshape(bt, L, N_SSM_GROUPS, SSM_GROUP).astype(jnp.complex64)
    y_f = s5_direction(u_c, lam_re[0], lam_im[0], b_re[0], b_im[0], c_re[0], c_im[0], log_dt[0])
    y_b = jnp.flip(s5_direction(jnp.flip(u_c, axis=1), lam_re[1], lam_im[1], b_re[1], b_im[1],
                                c_re[1], c_im[1], log_dt[1]), axis=1)
    y = (y_f + y_b).reshape(bt, L, D_SSM) + d_skip.astype(jnp.float32) * u32
    y = jax.nn.gelu(y)
    y = y * jax.nn.sigmoid(y @ w_glu.astype(jnp.float32) + b_glu.astype(jnp.float32))
    return y.astype(u.dtype)


def neighbourhood_attention(q, k, v, rpb):
    bt, L, H, dh = q.shape
    rows = L // GRID_W
    kh = min(WIN_H, rows)
    r = np.arange(rows)
    row_start = np.clip(r - kh // 2, 0, rows - kh)
    rows_idx = row_start[:, None] + np.arange(kh)[None, :]
    row_off = rows_idx - r[:, None]
    cb_start = np.clip(np.arange(N_COL_BLOCKS) * Q_BLOCK_W - WIN_W // 2, 0, GRID_W - K_BLOCK_W)
    col_idx = cb_start[:, None] + np.arange(K_BLOCK_W)[None, :]
    qcol = np.arange(GRID_W).reshape(N_COL_BLOCKS, Q_BLOCK_W)
    q_start = np.clip(qcol - WIN_W // 2, 0, GRID_W - WIN_W)
    kcol = col_idx[:, None, :]
    valid = (kcol >= q_start[..., None]) & (kcol < q_start[..., None] + WIN_W)
    col_off = kcol - qcol[..., None]
    ri = row_off + (WIN_H - 1)
    ci = np.clip(col_off, -(WIN_W - 1), WIN_W - 1) + (WIN_W - 1)
    qg = q.reshape(bt, rows, N_COL_BLOCKS, Q_BLOCK_W, H, dh)
    kg = k.reshape(bt, rows, GRID_W, H, dh)
    vg = v.reshape(bt, rows, GRID_W, H, dh)
    gr = rows_idx[:, None, :, None]
    gc = col_idx[None, :, None, :]
    k_blk = kg[:, gr, gc]
    v_blk = vg[:, gr, gc]
    s = jnp.einsum('brcqhd,brckwhd->brchqkw', qg, k_blk).astype(jnp.float32) * (dh ** -0.5)
    bias = rpb.astype(jnp.float32)[:, ri[:, None, None, :, None], ci[None, :, :, None, :]]
    s = s + jnp.transpose(bias, (1, 2, 0, 3, 4, 5))[None]
    s = jnp.where(valid[None, None, :, None, :, None, :], s, -jnp.inf)
    shp = s.shape
    p = jax.nn.softmax(s.reshape(shp[:-2] + (kh * K_BLOCK_W,)), axis=-1).reshape(shp)
    o = jnp.einsum('brchqkw,brckwhd->brcqhd', p, v_blk.astype(jnp.float32))
    return o.reshape(bt, L, H, dh).astype(q.dtype)


def hybrid_layer(x, norm_g, w_in, lam_re, lam_im, b_re, b_im, c_re, c_im, log_dt,
                 d_skip, w_glu, b_glu, rpb, ssm_out_g, attn_out_g, w_out):
    bt, L, _ = x.shape
    h = rmsnorm(x, norm_g)
    proj = h @ w_in
    cuts = list(np.cumsum(SPLITS)[:-1])
    u_s, z_s, q, k, v, z_a = jnp.split(proj, cuts, axis=-1)
    y_s = s5_mixer(u_s, lam_re, lam_im, b_re, b_im, c_re, c_im, log_dt, d_skip, w_glu, b_glu)
    y_s = rmsnorm(y_s, ssm_out_g) * jax.nn.silu(z_s)
    y_a = neighbourhood_attention(q.reshape(bt, L, N_HEADS, HEAD_DIM),
                                  k.reshape(bt, L, N_HEADS, HEAD_DIM),
                                  v.reshape(bt, L, N_HEADS, HEAD_DIM), rpb).reshape(bt, L, D_ATTN)
    y_a = rmsnorm(y_a, attn_out_g) * jax.nn.silu(z_a)
    mixed = jnp.concatenate([y_s, y_a], axis=-1)
    return x + (mixed @ w_out).astype(x.dtype)


def trunk(x, norm_g, w_in, lam_re, lam_im, b_re, b_im, c_re, c_im, log_dt,
          d_skip, w_glu, b_glu, rpb, ssm_out_g, attn_out_g, w_out, final_norm_g):
    for i in range(DEPTH):
        x = hybrid_layer(x, norm_g[i], w_in[i], lam_re[i], lam_im[i], b_re[i], b_im[i],
                         c_re[i], c_im[i], log_dt[i], d_skip[i], w_glu[i], b_glu[i],
                         rpb[i], ssm_out_g[i], attn_out_g[i], w_out[i])
    return rmsnorm(x, final_norm_g)


def setup_inputs(seed: int = 0) -> dict:
    key = jax.random.key(seed)
    ks = jax.random.split(key, 20)
    f32 = jnp.float32
    G, P, C = N_SSM_GROUPS, STATE_P, SSM_GROUP
    x_prompt = jax.random.normal(ks[0], (BATCH, SEQ, D_MODEL), f32)
    x_sample = jax.random.normal(ks[1], (DEC_BATCH, DEC_SEQ, D_MODEL), f32)
    norm_g = 1.0 + 0.02 * jax.random.normal(ks[2], (DEPTH, D_MODEL), f32)
    w_in = jax.random.normal(ks[3], (DEPTH, D_MODEL, D_IN_PROJ), f32) * D_MODEL ** -0.5
    lam_re = -0.5 + 0.01 * jax.random.normal(ks[4], (DEPTH, 2, G, P), f32)
    lam_im = (np.pi * jnp.arange(P, dtype=f32))[None, None, None, :] + 0.01 * jax.random.normal(ks[5], (DEPTH, 2, G, P), f32)
    b_re = jax.random.normal(ks[6], (DEPTH, 2, G, P, C), f32) * (2 * C) ** -0.5
    b_im = jax.random.normal(ks[7], (DEPTH, 2, G, P, C), f32) * (2 * C) ** -0.5
    c_re = jax.random.normal(ks[8], (DEPTH, 2, G, C, P), f32) * P ** -0.5
    c_im = jax.random.normal(ks[9], (DEPTH, 2, G, C, P), f32) * P ** -0.5
    log_dt = jax.random.uniform(ks[10], (DEPTH, 2, G), f32, math.log(DT_MIN), math.log(DT_MAX))
    d_skip = jax.random.normal(ks[11], (DEPTH, D_SSM), f32)
    w_glu = jax.random.normal(ks[12], (DEPTH, D_SSM, D_SSM), f32) * D_SSM ** -0.5
    b_glu = 0.01 * jax.random.normal(ks[13], (DEPTH, D_SSM), f32)
    rpb = 0.02 * jax.random.normal(ks[14], (DEPTH, N_HEADS, 2 * WIN_H - 1, 2 * WIN_W - 1), f32)
    ssm_out_g = 1.0 + 0.02 * jax.random.normal(ks[15], (DEPTH, D_SSM), f32)
    attn_out_g = 1.0 + 0.02 * jax.random.normal(ks[16], (DEPTH, D_ATTN), f32)
    w_out = jax.random.normal(ks[17], (DEPTH, D_MIX, D_MODEL), f32) * D_MIX ** -0.5
    final_norm_g = 1.0 + 0.02 * jax.random.normal(ks[18], (D_MODEL,), f32)
    return {'x_prompt': x_prompt, 'x_sample': x_sample, 'norm_g': norm_g, 'w_in': w_in,
            'lam_re': lam_re, 'lam_im': lam_im, 'b_re': b_re, 'b_im': b_im,
            'c_re': c_re, 'c_im': c_im, 'log_dt': log_dt, 'd_skip': d_skip,
            'w_glu': w_glu, 'b_glu': b_glu, 'rpb': rpb, 'ssm_out_g': ssm_out_g,
            'attn_out_g': attn_out_g, 'w_out': w_out, 'final_norm_g': final_norm_g}


def reference(x_prompt, x_sample, norm_g, w_in, lam_re, lam_im, b_re, b_im, c_re, c_im,
              log_dt, d_skip, w_glu, b_glu, rpb, ssm_out_g, attn_out_g, w_out, final_norm_g):
    y_prompt = trunk(x_prompt, norm_g, w_in, lam_re, lam_im, b_re, b_im, c_re, c_im, log_dt,
                     d_skip, w_glu, b_glu, rpb, ssm_out_g, attn_out_g, w_out, final_norm_g)
    y_sample = trunk(x_sample, norm_g, w_in, lam_re, lam_im, b_re, b_im, c_re, c_im, log_dt,
                     d_skip, w_glu, b_glu, rpb, ssm_out_g, attn_out_g, w_out, final_norm_g)
    return (y_prompt, y_sample)
```

```python
import contextlib
import numpy as np
import ml_dtypes
import concourse.bass as bass
import concourse.mybir as mybir
from concourse.bass_utils import run_bass_kernel_spmd

F32 = mybir.dt.float32
BF16 = mybir.dt.bfloat16
ALU = mybir.AluOpType
AF = mybir.ActivationFunctionType
AX = mybir.AxisListType

COMPUTE = ("pe", "act", "dve", "pool")


class Prog:
    def __init__(self, nc, n_dma_sems=(20, 12)):
        self.nc = nc
        self.ops = []
        self.streams = {e: [] for e in ("pe", "act", "dve", "pool", "sp")}
        self.last_w = {}
        self.readers = {}
        self.n_dma_sems = {"sp": n_dma_sems[0], "pool": n_dma_sems[1], "act": 0}
        self.dma_count = {"sp": 0, "pool": 0}
        self.dma_last_on_sem = {}
        self.barrier_deps = set()
        self.since_barrier = set()

    def add(self, eng, fn, reads=(), writes=(), dma=False, extra_deps=()):
        oid = len(self.ops)
        op = dict(id=oid, eng=eng, fn=fn, dma=dma, deps=set(extra_deps), signals=dma, token=None)
        deps = op["deps"]
        deps |= self.barrier_deps
        for k in reads:
            w = self.last_w.get(k)
            if w is not None:
                wo = self.ops[w]
                if not (eng == "pe" and wo["eng"] == "pe" and not dma):
                    deps.add(w)
        for k in writes:
            w = self.last_w.get(k)
            if w is not None:
                wo = self.ops[w]
                if not (eng == "pe" and wo["eng"] == "pe" and not dma and not wo["dma"]):
                    deps.add(w)
            for r in self.readers.get(k, ()):
                ro = self.ops[r]
                if not (eng == "pe" and ro["eng"] == "pe" and not dma and not ro["dma"]):
                    deps.add(r)
        for k in reads:
            lst = self.readers.setdefault(k, [])
            if not dma:
                lst[:] = [r for r in lst if self.ops[r]["dma"] or self.ops[r]["eng"] != eng]
            lst.append(oid)
        for k in writes:
            self.last_w[k] = oid
            self.readers[k] = []
        if dma:
            n = self.dma_count[eng]
            self.dma_count[eng] = n + 1
            slot = (eng, n % self.n_dma_sems[eng])
            prev = self.dma_last_on_sem.get(slot)
            if prev is not None:
                deps.add(prev)
                val = self.ops[prev]["token"][1] + 16
            else:
                val = 16
            op["token"] = (slot, val)
            self.dma_last_on_sem[slot] = oid
        deps.discard(oid)
        self.ops.append(op)
        self.streams[eng].append(oid)
        if dma:
            self.since_barrier.add(oid)
        return oid

    def barrier(self):
        deps = set(self.since_barrier)
        for e, s in self.streams.items():
            if s:
                deps.add(s[-1])
        for e in self.streams:
            for oid in reversed(self.streams[e]):
                if not self.ops[oid]["dma"]:
                    deps.add(oid)
                    break
        self.barrier_deps = deps
        self.since_barrier = set()

    def pe(self, fn, r=(), w=(), **kw):
        return self.add("pe", fn, r, w, **kw)

    def act(self, fn, r=(), w=(), **kw):
        return self.add("act", fn, r, w, **kw)

    def dve(self, fn, r=(), w=(), **kw):
        return self.add("dve", fn, r, w, **kw)

    def pool(self, fn, r=(), w=(), **kw):
        return self.add("pool", fn, r, w, **kw)

    def dma(self, fn, r=(), w=(), q="sp", **kw):
        return self.add(q, fn, r, w, dma=True, **kw)

    def emit(self):
        nc = self.nc
        ops = self.ops
        for op in ops:
            for d in op["deps"]:
                ops[d]["signals"] = True
        cnt = {e: 0 for e in COMPUTE}
        for op in ops:
            if not op["dma"] and op["signals"]:
                cnt[op["eng"]] += 1
                op["token"] = (op["eng"], cnt[op["eng"]])
        import contextlib
        with contextlib.ExitStack() as st:
            sems = {}
            for e in COMPUTE:
                sems[e] = st.enter_context(nc.semaphore("c_" + e))
            for q in ("sp", "pool"):
                for i in range(self.n_dma_sems[q]):
                    sems[(q, i)] = st.enter_context(nc.semaphore("d_%s_%d" % (q, i)))
            block = st.enter_context(nc.Block())
            streams = self.streams

            def run(eng_name, handle):
                waited = {}
                for oid in streams[eng_name]:
                    op = ops[oid]
                    need = {}
                    for d in op["deps"]:
                        s, v = ops[d]["token"]
                        if v > need.get(s, 0):
                            need[s] = v
                    for s, v in need.items():
                        if waited.get(s, 0) < v:
                            handle.wait_ge(sems[s], v)
                            waited[s] = v
                    f = op["fn"]
                    if isinstance(f, tuple):
                        ins = getattr(handle, f[0])(*f[1], **f[2])
                    else:
                        ins = f(handle)
                    if op["dma"]:
                        ins.then_inc(sems[op["token"][0]], 16)
                    elif op["signals"]:
                        ins.then_inc(sems[op["eng"]], 1)
                if eng_name in ("sp", "pool"):
                    for i in range(self.n_dma_sems[eng_name]):
                        last = self.dma_last_on_sem.get((eng_name, i))
                        if last is not None:
                            v = ops[last]["token"][1]
                            if waited.get((eng_name, i), 0) < v:
                                handle.wait_ge(sems[(eng_name, i)], v)

            @block.tensor
            def _(e):
                run("pe", e)

            @block.scalar
            def _(e):
                run("act", e)

            @block.vector
            def _(e):
                run("dve", e)

            @block.gpsimd
            def _(e):
                run("pool", e)

            @block.sync
            def _(e):
                run("sp", e)
        self.sem_counts = cnt


def I(name, *a, **k):
    return (name, a, k)


NBUF = 2560
OWN0 = 256
NOWN = 2048
EPS = 1e-6
TWO_PI = 6.283185307179586
MAGIC = 12582912.0


def build_program():
    nc = bass.Bass("TRN2", target_bir_lowering=False)
    DI = lambda name, shape, dt=F32: nc.dram_tensor(name, shape, dt, kind="ExternalInput").ap()
    xc = DI("xc", [2, NBUF, 1024])
    w_in = DI("w_in", [1024, 3072]); w_out = DI("w_out", [1024, 1024]); w_glu = DI("w_glu", [512, 512])
    ng_d = DI("ng", [128, 1024]); fg_d = DI("fg", [128, 1024]); ag_d = DI("ag", [128, 512])
    sgc_d = DI("sgc", [128, 4]); bgc_d = DI("bgc", [128, 4]); dskc_d = DI("dskc", [128, 4])
    lamr_d = DI("lamr", [128, 80]); lami_d = DI("lami", [128, 80]); logdt_d = DI("logdt", [128, 80])
    brec_d = DI("brec", [128, 768]); bimc_d = DI("bimc", [128, 768]); lsel_d = DI("lsel", [128, 9])
    bre_d = DI("bre", [128, 512]); bim_d = DI("bim", [128, 512]); cre_d = DI("cre", [128, 512]); cim_d = DI("cim", [128, 512])
    rpbp_d = DI("rpbp", [8, 16, 127])
    cv_d = DI("cv", [128, 64], BF16); rvi_d = DI("rvi", [128, 10], BF16); rvc_d = DI("rvc", [128, 2 * 4 * 14], BF16)
    kex_d = DI("kex", [128, 40])
    iota_d = DI("iota1", [128, 512])
    xoth = DI("xoth", [6144, 1024])
    yo = nc.dram_tensor("yo", [2, NOWN, 1024], F32, kind="ExternalOutput").ap()
    winb = nc.dram_tensor("winb", [1024, 3072], BF16)
    woutb = nc.dram_tensor("woutb", [1024, 1024], BF16)
    ccin = nc.dram_tensor("ccin", [128, 64], F32)
    ccout = nc.dram_tensor("ccout", [1024, 64], F32)

    with contextlib.ExitStack() as st:
        def SB(name, shape, dt=F32):
            return st.enter_context(nc.sbuf_tensor(name, shape, dt))
        banks = [st.enter_context(nc.psum_tensor("bank%d" % i, [128, 512], F32)) for i in range(8)]
        cc_sem = st.enter_context(nc.semaphore("ccs"))
        P = Prog(nc)

        ident = SB("ident", [128, 128], BF16)
        identf = SB("identf", [128, 128], F32)
        ones_c = SB("ones_c", [128, 1], BF16)
        epsc = SB("epsc", [128, 1], F32)
        cM0 = SB("cM0", [128, 1], F32); cM1 = SB("cM1", [128, 1], F32); cMn = SB("cMn", [128, 1], F32); cHP = SB("cHP", [128, 1], F32)
        iota1 = SB("iota1s", [128, 512], F32)
        sgc = SB("sgcs", [128, 4]); bgc = SB("bgcs", [128, 4]); dskc = SB("dskcs", [128, 4])
        kex = SB("kexs", [128, 40]); cv = SB("cvs", [128, 64], BF16)
        rvi = SB("rvis", [128, 10], BF16); rvc = SB("rvcs", [128, 112], BF16)
        ag = SB("ags", [128, 512]); gvec = SB("gvec", [128, 1024])
        WGB = SB("WGB", [128, 4, 512], BF16)
        TH = SB("TH", [128, 80]); LR = SB("LR", [128, 80]); RC = SB("RC", [128, 80])
        HKc = SB("HKc", [128, 4, 32, 2])
        SPK = SB("SPK", [128, 64])
        HLc = SB("HLc", [128, 2, 16]); HFB = SB("HFB", [128, 2, 32]); LSEL = SB("LSEL", [128, 9]); TL1 = SB("TL1", [128, 2]); TL2 = SB("TL2", [128, 2]); TL3 = SB("TL3", [128, 2]); TL4 = SB("TL4", [128, 2])
        R_BC = SB("R_BC", [128, 16384], BF16)
        BmT = R_BC[:, 0:8192].rearrange("p (m c) -> p m c", c=128)
        CmT = R_BC[:, 8192:16384].rearrange("p (m c) -> p m c", c=128)
        MB = R_BC[:, 0:7168].rearrange("p (h o c) -> p h o c", h=8, o=7)
        MBI = R_BC[:, 7168:12288].rearrange("p (h o c) -> p h o c", h=8, o=5)
        R_A = SB("R_A", [128, 8192], F32)
        hT = R_A[:].bitcast(BF16).rearrange("p (k t) -> p k t", k=8)
        Y0T = R_A[:].rearrange("p (k t) -> p k t", k=4)
        R_B = SB("R_B", [128, 32768], BF16)
        qT = R_B[:, 0:4096].rearrange("p (k t) -> p k t", k=4)
        kT = R_B[:, 4096:10240].rearrange("p (k t) -> p k t", k=4)
        VA = R_B[:, 10240:16480].rearrange("p (t h c) -> p t h c", t=12, h=8)
        ZAG = R_B[:, 16480:20576].rearrange("p (t c) -> p t c", t=8)
        RPE = R_B[:, 20576:28256].rearrange("p (h r c) -> p h r c", h=8, r=15)
        uT = R_B[:, 0:8192].rearrange("p (k t) -> p k t", k=4)
        ZSG = R_B[:, 8192:16384].rearrange("p (k t) -> p k t", k=4)
        YGT = R_B[:, 16384:24576].rearrange("p (k t) -> p k t", k=4)
        Y2Q = R_B[:, 24576:32768].rearrange("p (k t) -> p k t", k=4)
        MAT = SB("MAT", [128, 4, 2048], BF16)
        RSTD_S = SB("RSTD_S", [128, 16])
        WS = [SB("WS%d" % i, [128, 8, 512], BF16) for i in range(2)]
        stg = [SB("stg%d" % i, [128, 1024], F32) for i in range(2)]
        scr = SB("scr", [128, 1024], F32)
        hn = [SB("hn%d" % i, [128, 1024], BF16) for i in range(2)]
        colt = SB("colt", [128, 64], F32)
        colctr = [0]

        def newcol():
            colctr[0] = (colctr[0] + 1) % 64
            return colctr[0]

        def ld(dst, src, key, q="sp"):
            P.dma(I("dma_start", out=dst, in_=src), w=[key], q=q)
        ld(iota1[:], iota_d, "iota1"); ld(sgc[:], sgc_d, "sgc"); ld(bgc[:], bgc_d, "bgc"); ld(dskc[:], dskc_d, "dskc")
        ld(LSEL[:], lsel_d, "LSEL"); ld(kex[:], kex_d, "kex"); ld(cv[:], cv_d, "cv"); ld(rvi[:], rvi_d, "rvi"); ld(rvc[:], rvc_d, "rvc"); ld(ag[:], ag_d, "ag")
        P.pool(I("memset", identf[:], 1.0), w=["identf"])
        P.pool(I("affine_select", identf[:], identf[:], [[-1, 128]], ALU.is_equal, 0.0, base=0, channel_multiplier=1), r=["identf"], w=["identf"])
        P.pool(I("tensor_copy", ident[:], identf[:]), r=["identf"], w=["ident"])
        P.pool(I("memset", ones_c[:], 1.0), w=["ones_c"])
        P.pool(I("memset", epsc[:], EPS), w=["epsc"])
        P.pool(I("memset", cM0[:], MAGIC), w=["cM"])
        P.pool(I("memset", cM1[:], MAGIC + 0.25), w=["cM"])
        P.pool(I("memset", cMn[:], -MAGIC), w=["cM"])
        P.pool(I("memset", cHP[:], TWO_PI / 4), w=["cM"])

        lamr = SB("lamr_s", [128, 80]); lami = SB("lami_s", [128, 80]); tmpa = SB("tmpa", [128, 80]); tmpb = SB("tmpb", [128, 80])
        tmpc = SB("tmpc", [128, 80]); tmpd = SB("tmpd", [128, 80]); CR = SB("CR", [128, 80]); CI = SB("CI", [128, 80])
        ld(lamr[:], lamr_d, "lamr"); ld(lami[:], lami_d, "lami"); ld(tmpa[:], logdt_d, "tmpa")
        cast_ct = [0]

        def precast(src, dst, ncols, tagname):
            for kc in range(8 if src is not w_glu else 4):
                for c0 in range(0, ncols, 1024):
                    cw = min(1024, ncols - c0)
                    i = cast_ct[0] % 2; cast_ct[0] += 1
                    P.dma(I("dma_start", out=stg[i][:, 0:cw], in_=src[kc * 128:(kc + 1) * 128, c0:c0 + cw]), w=[("stg", i)])
                    if dst is None:
                        P.act(I("activation", WGB[:, kc, 0:cw], stg[i][:, 0:cw], AF.Copy), r=[("stg", i)], w=["WGB"])
                    else:
                        P.act(I("activation", hn[i][:, 0:cw], stg[i][:, 0:cw], AF.Copy), r=[("stg", i)], w=[("hn", i)])
                        P.dma(I("dma_start", out=dst.ap()[kc * 128:(kc + 1) * 128, c0:c0 + cw], in_=hn[i][:, 0:cw]), r=[("hn", i)], w=[tagname], q="pool")
        precast(w_in, winb, 3072, "winb")
        precast(w_out, woutb, 1024, "woutb")
        precast(w_glu, None, 512, "wglu")

        V_ = lambda fn, r, w: P.dve(fn, r, w)
        P.act(I("activation", tmpa[:], tmpa[:], AF.Exp), r=["tmpa"], w=["tmpa"])
        V_(I("tensor_tensor", LR[:], lamr[:], tmpa[:], ALU.mult), ["lamr", "tmpa"], ["LR"])
        V_(I("tensor_tensor", TH[:], lami[:], tmpa[:], ALU.mult), ["lami", "tmpa"], ["TH"])
        P.act(I("activation", RC[:], LR[:], AF.Exp), r=["LR"], w=["RC"])

        def reduce_angle(dst, src_ap, shift, rk, wk, eng="dve", tmp=None, tk=None):
            add = P.dve if eng == "dve" else P.pool
            add(I("tensor_scalar", dst, src_ap, shift, None, ALU.add), r=rk, w=[wk])
            add(I("tensor_scalar", tmp, dst, 1.0 / TWO_PI, MAGIC, ALU.mult, ALU.add), r=[wk], w=[tk])
            add(I("tensor_scalar", tmp, tmp, -MAGIC, -TWO_PI, ALU.add, ALU.mult), r=[tk], w=[tk])
            add(I("tensor_tensor", dst, tmp, dst, ALU.add), r=[tk, wk], w=[wk])
            add(I("tensor_scalar", dst, dst, -3.14159, 3.14159, ALU.max, ALU.min), r=[wk], w=[wk])

        def power(n, outr, outi, tag):
            P.dve(I("tensor_scalar", tmpb[:], TH[:], float(n), None, ALU.mult), r=["TH"], w=["tmpb"])
            reduce_angle(tmpc[:], tmpb[:], 0.0, ["tmpb"], "tmpc", tmp=tmpd[:], tk="tmpd")
            P.act(I("activation", outi, tmpc[:], AF.Sin, scale=0.999996), r=["tmpc"], w=[tag + "i"])
            reduce_angle(tmpc[:], tmpb[:], TWO_PI / 4, ["tmpb"], "tmpc", tmp=tmpd[:], tk="tmpd")
            P.act(I("activation", outr, tmpc[:], AF.Sin, scale=0.999996), r=["tmpc"], w=[tag + "r"])
            P.act(I("activation", tmpc[:], LR[:], AF.Exp, scale=float(n)), r=["LR", tag + "r", tag + "i"], w=["tmpc"])
            P.dve(I("tensor_tensor", outr, outr, tmpc[:], ALU.mult), r=[tag + "r", "tmpc"], w=[tag + "r"])
            P.dve(I("tensor_tensor", outi, outi, tmpc[:], ALU.mult), r=[tag + "i", "tmpc"], w=[tag + "i"])

        A1R = SB("A1R", [128, 80]); A1I = SB("A1I", [128, 80])
        power(1, A1R[:], A1I[:], "A1")
        V_(I("tensor_scalar", tmpa[:], A1R[:], -1.0, None, ALU.add), ["A1r"], ["tmpa"])
        V_(I("tensor_tensor", tmpb[:], lamr[:], lamr[:], ALU.mult), ["lamr"], ["tmpb"])
        V_(I("tensor_tensor", tmpc[:], lami[:], lami[:], ALU.mult), ["lami"], ["tmpc"])
        V_(I("tensor_tensor", tmpb[:], tmpb[:], tmpc[:], ALU.add), ["tmpb", "tmpc"], ["tmpb"])
        V_(I("reciprocal", tmpb[:], tmpb[:]), ["tmpb"], ["tmpb"])
        V_(I("tensor_tensor", tmpc[:], tmpa[:], lamr[:], ALU.mult), ["tmpa", "lamr", "tmpb"], ["tmpc"])
        V_(I("tensor_tensor", tmpd[:], A1I[:], lami[:], ALU.mult), ["A1i", "lami"], ["tmpd"])
        V_(I("tensor_tensor", tmpc[:], tmpc[:], tmpd[:], ALU.add), ["tmpc", "tmpd"], ["tmpc"])
        V_(I("tensor_tensor", CR[:], tmpc[:], tmpb[:], ALU.mult), ["tmpc", "tmpb"], ["CR"])
        V_(I("tensor_tensor", tmpc[:], A1I[:], lamr[:], ALU.mult), ["A1i", "lamr", "CR"], ["tmpc"])
        V_(I("tensor_tensor", tmpd[:], tmpa[:], lami[:], ALU.mult), ["tmpa", "lami", "tmpc"], ["tmpd"])
        V_(I("tensor_tensor", tmpc[:], tmpc[:], tmpd[:], ALU.subtract), ["tmpc", "tmpd"], ["tmpc"])
        V_(I("tensor_tensor", CI[:], tmpc[:], tmpb[:], ALU.mult), ["tmpc", "tmpb"], ["CI"])
        cb = lambda t: t[:, 0:32].unsqueeze(2).to_broadcast([128, 32, 16])
        T1 = SB("T1", [128, 512]); T2 = SB("T2", [128, 512])
        BBR = scr[:, 0:512]; BBI = scr[:, 512:1024]
        ld(BBR, bre_d, "scr"); ld(BBI, bim_d, "scr")
        b3 = lambda t: t.rearrange("p (m c) -> p m c", c=16)
        T3s = SB("T3s", [128, 512])
        V_(I("tensor_tensor", b3(T1[:]), b3(BBR), cb(CR), ALU.mult), ["scr", "CR"], ["T1"])
        V_(I("tensor_tensor", b3(T2[:]), b3(BBI), cb(CI), ALU.mult), ["scr", "CI"], ["T2"])
        V_(I("tensor_tensor", T1[:], T1[:], T2[:], ALU.subtract), ["T1", "T2"], ["T1"])
        V_(I("tensor_tensor", b3(T2[:]), b3(BBI), cb(CR), ALU.mult), ["scr", "CR", "T1"], ["T2"])
        V_(I("tensor_tensor", b3(T3s[:]), b3(BBR), cb(CI), ALU.mult), ["scr", "CI"], ["T3s"])
        V_(I("tensor_tensor", T2[:], T2[:], T3s[:], ALU.add), ["T2", "T3s"], ["T2"])
        ZB = R_A[:, 0:2048].rearrange("p (g c) -> p g c", c=128)

        def build_bc():
            P.dve(I("memset", R_BC[:, 8192:16384], 0.0), w=["CmT"])
            ld(scr[:, 0:512], cre_d, "scr"); ld(scr[:, 512:1024], cim_d, "scr")
            P.dve(I("tensor_scalar", scr[:, 512:1024], scr[:, 512:1024], -1.0, None, ALU.mult), r=["scr"], w=["scr"])
            srcs = {(0, "B"): T1[:], (1, "B"): T2[:], (0, "C"): scr[:, 0:512], (1, "C"): scr[:, 512:1024]}
            for d in range(2):
                for ri in range(2):
                    m0 = (d * 2 + ri) * 16
                    src = srcs[(ri, "C")].rearrange("p (m c) -> p m c", c=16)
                    for j2 in range(2):
                        for gq in range(4):
                            P.dve(I("tensor_copy",
                                CmT[64 * j2:64 * j2 + 64, m0 + gq:m0 + 16:4, 32 * gq + 16 * j2:32 * gq + 16 * j2 + 16],
                                src[64 * j2:64 * j2 + 64, d * 16 + gq:d * 16 + 16:4, :]), r=["scr"], w=["CmT"])
                    srcb = srcs[(ri, "B")].rearrange("p (m c) -> p m c", c=16)
                    P.dve(I("memset", ZB, 0.0), w=["ZB"])
                    for j2 in range(2):
                        for gq in range(4):
                            P.dve(I("tensor_copy",
                                ZB[64 * j2:64 * j2 + 64, gq:16:4, 32 * gq + 16 * j2:32 * gq + 16 * j2 + 16],
                                srcb[64 * j2:64 * j2 + 64, d * 16 + gq:d * 16 + 16:4, :]), r=["T1", "T2"], w=["ZB"])
                    for g4 in range(4):
                        bk = banks[g4 % 2]
                        for j in range(4):
                            gp = g4 * 4 + j
                            P.pe(I("transpose", bk[:, j * 128:(j + 1) * 128], ZB[:, gp, :], identf[:]), r=["ZB", "identf"], w=[("bank", g4 % 2)])
                        P.act(I("activation", BmT[:, m0 + g4 * 4:m0 + g4 * 4 + 4, :], bk[:].rearrange("p (j c) -> p j c", c=128), AF.Copy), r=[("bank", g4 % 2)], w=["BmT"])

        def build_bmats(src_ri, nsel, dst3d, rkeys, wkey):
            for sel in range(nsel):
                for ri in range(2):
                    m0 = (sel * 2 + ri) * 16
                    srcb = src_ri[ri]
                    P.dve(I("memset", ZB, 0.0), w=["ZB"])
                    for j2 in range(2):
                        for gq in range(4):
                            P.dve(I("tensor_copy",
                                ZB[64 * j2:64 * j2 + 64, gq:16:4, 32 * gq + 16 * j2:32 * gq + 16 * j2 + 16],
                                srcb[64 * j2:64 * j2 + 64, sel * 16 + gq:sel * 16 + 16:4, :]), r=rkeys, w=["ZB"])
                    for g4 in range(4):
                        bk = banks[g4 % 2]
                        for j in range(4):
                            gp = g4 * 4 + j
                            P.pe(I("transpose", bk[:, j * 128:(j + 1) * 128], ZB[:, gp, :], identf[:]), r=["ZB", "identf"], w=[("bank", g4 % 2)])
                        P.act(I("activation", dst3d[:, m0 + g4 * 4:m0 + g4 * 4 + 4, :], bk[:].rearrange("p (j c) -> p j c", c=128), AF.Copy), r=[("bank", g4 % 2)], w=[wkey])

        BmTc = R_BC[:, 0:12288].rearrange("p (m c) -> p m c", c=128)

        def build_bc_carry():
            f = lambda o: R_A[:, 2048 + 768 * o:2048 + 768 * (o + 1)]
            BRc, BIc, U1, U2, U3 = f(0), f(1), f(2), f(3), f(4)
            ld(BRc, brec_d, "BRc"); ld(BIc, bimc_d, "BIc")
            c3 = lambda t: t.rearrange("p (m c) -> p m c", c=16)
            cbc = lambda t: t[:, 32:80].unsqueeze(2).to_broadcast([128, 48, 16])
            P.dve(I("tensor_tensor", c3(U1), c3(BRc), cbc(CR), ALU.mult), r=["BRc", "CR"], w=["U1"])
            P.dve(I("tensor_tensor", c3(U2), c3(BIc), cbc(CI), ALU.mult), r=["BIc", "CI"], w=["U2"])
            P.dve(I("tensor_tensor", U1, U1, U2, ALU.subtract), r=["U1", "U2"], w=["U1"])
            P.dve(I("tensor_tensor", c3(U2), c3(BIc), cbc(CR), ALU.mult), r=["BIc", "CR", "U1"], w=["U2"])
            P.dve(I("tensor_tensor", c3(U3), c3(BRc), cbc(CI), ALU.mult), r=["BRc", "CI"], w=["U3"])
            P.dve(I("tensor_tensor", U2, U2, U3, ALU.add), r=["U2", "U3"], w=["U2"])
            build_bmats([c3(U1), c3(U2)], 3, BmTc, ["U1", "U2"], "BmTc")

        def rstd_from_ss(ss_ap, n, key_r, key_w):
            P.act(I("activation", ss_ap, ss_ap, AF.Sqrt, bias=epsc[:, 0:1], scale=1.0 / n), r=[key_r, "epsc"], w=[key_r])
            P.dve(I("reciprocal", ss_ap, ss_ap), r=[key_r], w=[key_r])

        tilectr = [0]

        def norm_tiles(ci, tok0, ntiles, src=None, t0=0, bankbase=6):
            src = xc[ci] if src is None else src
            for t in range(ntiles):
                i = tilectr[0] % 2; tilectr[0] += 1
                c = newcol()
                P.dma(I("dma_start", out=stg[i][:], in_=src[tok0 + t * 128:tok0 + (t + 1) * 128, :]), w=[("stg", i)])
                P.act(I("activation", hn[i][:], stg[i][:], AF.Square), r=[("stg", i)], w=[("hn", i)])
                P.dve(I("reduce_sum", colt[:, c:c + 1], hn[i][:], axis=AX.X), r=[("hn", i)], w=[("col", c)])
                rstd_from_ss(colt[:, c:c + 1], 1024.0, ("col", c), ("col", c))
                P.dve(I("scalar_tensor_tensor", hn[i][:], stg[i][:], colt[:, c:c + 1], gvec[:], ALU.mult, ALU.mult),
                      r=[("stg", i), ("col", c), "gvec"], w=[("hn", i)])
                bn = bankbase + ((t + t0) % 2)
                bk = banks[bn]
                bkb = bk[:].bitcast(BF16)
                for kc in range(8):
                    P.pe(I("transpose", bkb[:, kc * 128:(kc + 1) * 128], hn[i][:, kc * 128:(kc + 1) * 128], ident[:]),
                         r=[("hn", i), "ident"], w=[("bank", bn)])
                P.act(I("activation", hT[:, :, (t + t0) * 128:(t + t0 + 1) * 128], bkb.rearrange("p (k c) -> p k c", c=128), AF.Copy),
                      r=[("bank", bn)], w=[("hT", t + t0)])

        wsctr = [0]

        def load_wgroup(cg):
            i = wsctr[0] % 2; wsctr[0] += 1
            P.dma(I("dma_start", out=WS[i][:], in_=winb.ap()[:, cg * 512:(cg + 1) * 512].rearrange("(k p) c -> p k c", p=128)),
                  r=["winb"], w=[("WS", i)])
            return i

        pbank = [0]

        def fm_proj(cg, tok_list, evac):
            wi = load_wgroup(cg)
            for cc in range(4):
                for (h0, n, dst0) in tok_list:
                    b = pbank[0] % 4; pbank[0] += 1
                    bk = banks[b]
                    tl = list(range(h0 // 128, (h0 + n + 127) // 128))
                    for kc in range(8):
                        P.pe(I("matmul", bk[:, 0:n], WS[wi][:, kc, cc * 128:(cc + 1) * 128], hT[:, kc, h0:h0 + n], start=(kc == 0), stop=(kc == 7)),
                             r=[("WS", wi)] + [("hT", t) for t in tl], w=[("bank", b)])
                    evac(cc, bk, b, n, dst0)

        def tm_proj(cg, tiles, evac):
            wi = load_wgroup(cg)
            for t in tiles:
                b = pbank[0] % 4; pbank[0] += 1
                bk = banks[b]
                for kc in range(8):
                    P.pe(I("matmul", bk[:], hT[:, kc, t * 128:(t + 1) * 128], WS[wi][:, kc, :], start=(kc == 0), stop=(kc == 7)),
                         r=[("WS", wi), ("hT", t)], w=[("bank", b)])
                evac(t, bk, b)

        sigt = [SB("sigt%d" % i, [128, 512], BF16) for i in range(2)]
        sgctr = [0]

        def build_masks():
            RP = R_A[:, 0:7680].rearrange("p (h r c) -> p h r c", h=8, r=15)
            for a in range(2):
                for h in range(8):
                    src = bass.AP(rpbp_d.tensor, (h * 16 + a) * 127, [[1, 64], [127, 15], [1, 64]])
                    P.dma(I("dma_start", out=RP[64 * a:64 * a + 64, h, :, :], in_=src), w=[("RP", a, h)])
            P.act(I("activation", R_B[:, 20576:28256], R_A[:, 0:7680], AF.Exp), r=[("RP", a, h) for a in range(2) for h in range(8)], w=["RPE"])
            for b in range(2):
                r0 = 1 - b
                P.dve(I("tensor_tensor", MB[:, :, :, b * 64:(b + 1) * 64], RPE[:, :, r0:r0 + 13:2, ::-1],
                                                             cv[:].unsqueeze(1).unsqueeze(1).to_broadcast([128, 8, 7, 64]), ALU.mult),
                      r=["RPE", "cv"], w=["MB"])
            P.dve(I("tensor_tensor", MBI.rearrange("p h o (b c) -> p h (o b) c", b=2), MB[:, :, 1:6, :].rearrange("p h o (b c) -> p h (o b) c", b=2),
                                            rvi[:].unsqueeze(1).unsqueeze(3).to_broadcast([128, 8, 10, 64]), ALU.mult),
                  r=["MB", "rvi"], w=["MBI"])

        PT = [R_B[:, 28256 + 768 * i:28256 + 768 * (i + 1)] for i in range(2)]
        YA = R_B[:, 29792:30816].bitcast(F32)
        MA = R_B[:, 30816:31328]
        YO = R_B[:, 31328:32368].bitcast(F32)
        RD = SB("RD", [128, 8], F32)

        def attention_half(ci, hh):
            base_tok = 1024 * hh
            P.dma(I("dma_start", out=gvec[:], in_=ng_d), w=["gvec"])
            norm_tiles(ci, base_tok, 12)
            def ev_q(cc, bk, b, n, dst0):
                P.act(I("activation", qT[:, cc, dst0:dst0 + n], bk[:, 0:n], AF.Copy, scale=0.125), r=[("bank", b)], w=[("qT", cc, dst0)])
            fm_proj(2, [(256, 512, 0), (768, 512, 512)], ev_q)
            def ev_k(cc, bk, b, n, dst0):
                P.dve(I("tensor_copy", kT[:, cc, dst0:dst0 + n], bk[:, 0:n]), r=[("bank", b)], w=[("kT", cc, dst0)])
            fm_proj(3, [(0, 512, 0), (512, 512, 512), (1024, 512, 1024)], ev_k)
            def ev_v(t, bk, b):
                P.act(I("activation", VA[:, t, :, 0:64], bk[:].rearrange("p (h c) -> p h c", c=64), AF.Copy), r=[("bank", b)], w=[("VA", t)])
                kt = ci * 20 + 8 * hh + t
                P.dve(I("tensor_copy", VA[:, t, :, 64:65], kex[:, kt:kt + 1].unsqueeze(1).to_broadcast([128, 8, 1])), r=["kex", ("VA", t)], w=[("VA", t)])
            tm_proj(4, list(range(12)), ev_v)
            def ev_z(t, bk, b):
                i = sgctr[0] % 2; sgctr[0] += 1
                P.act(I("activation", sigt[i][:], bk[:], AF.Sigmoid), r=[("bank", b)], w=[("sigt", i)])
                P.dve(I("tensor_tensor", scr[:, 0:512], bk[:], sigt[i][:], ALU.mult), r=[("bank", b), ("sigt", i)], w=["scr"])
                P.dve(I("tensor_tensor", ZAG[:, t - 2, :], scr[:, 0:512], ag[:], ALU.mult), r=["scr", "ag"], w=[("ZAG", t - 2)])
            tm_proj(5, list(range(2, 10)), ev_z)
            def geom(jpl):
                jp = 8 * hh + jpl
                olist = [-4, -2, 0, 2, 4]
                if jp == 0:
                    olist = olist + [6]
                if jp == 15:
                    olist = [-6] + olist
                return jp, olist, len(olist), jp in (0, 1, 14, 15), {0: 0, 1: 1, 14: 2, 15: 3}.get(jp, 0), (olist[0] + 6) // 2
            units = [(jpl, h) for jpl in range(8) for h in range(8)]

            def emit_S(ui):
                jpl, h = units[ui]
                jp, olist, n, boundary, jpb, os0 = geom(jpl)
                hp, hb = h // 2, 64 * (h % 2)
                bi = ui % 2
                sb0, sb1 = banks[2 * bi], banks[2 * bi + 1]
                for i, o in enumerate(olist):
                    kt = jpl + o // 2 + 2
                    dst = (sb0 if i < 4 else sb1)[:, (i % 4) * 128:(i % 4) * 128 + 128]
                    P.pe(I("matmul", dst, kT[hb:hb + 64, hp, kt * 128:(kt + 1) * 128], qT[hb:hb + 64, hp, jpl * 128:(jpl + 1) * 128], start=True, stop=True),
                         r=[("kT", hp, 512 * (kt // 4)), ("qT", hp, 512 * (jpl // 4))], w=[("bank", 2 * bi + (0 if i < 4 else 1))])

            def emit_softmax(ui):
                jpl, h = units[ui]
                jp, olist, n, boundary, jpb, os0 = geom(jpl)
                bi = ui % 2
                sb0, sb1 = banks[2 * bi], banks[2 * bi + 1]
                pt = PT[bi]
                P.act(I("activation", pt[:, 0:512], sb0[:], AF.Exp), r=[("bank", 2 * bi)], w=[("PT", bi)])
                P.act(I("activation", pt[:, 512:128 * n], sb1[:, 0:128 * (n - 4)], AF.Exp), r=[("bank", 2 * bi + 1)], w=[("PT", bi)])
                if not boundary:
                    P.dve(I("tensor_tensor", pt[:, 0:640], pt[:, 0:640], MBI[:, h, :, :].rearrange("p o c -> p (o c)"), ALU.mult),
                          r=[("PT", bi), "MBI"], w=[("PT", bi)])
                else:
                    P.dve(I("tensor_tensor", pt[:, 0:128 * n], pt[:, 0:128 * n], MB[:, h, os0:os0 + n, :].rearrange("p o c -> p (o c)"), ALU.mult),
                          r=[("PT", bi), "MB"], w=[("PT", bi)])
                    r0 = (ci * 4 + jpb) * 14 + 2 * os0
                    P.dve(I("tensor_tensor", pt[:, 0:128 * n].rearrange("p (x c) -> p x c", c=64), pt[:, 0:128 * n].rearrange("p (x c) -> p x c", c=64),
                            rvc[:, r0:r0 + 2 * n].unsqueeze(2).to_broadcast([128, 2 * n, 64]), ALU.mult),
                          r=[("PT", bi), "rvc"], w=[("PT", bi)])

            def emit_PV(ui):
                jpl, h = units[ui]
                jp, olist, n, boundary, jpb, os0 = geom(jpl)
                bi = ui % 2
                pt = PT[bi]
                ob = banks[4 + h // 4]
                for i, o in enumerate(olist):
                    kt = jpl + o // 2 + 2
                    P.pe(I("matmul", ob[:, (h % 4) * 65:(h % 4) * 65 + 65], pt[:, i * 128:(i + 1) * 128], VA[:, kt, h, :], start=(i == 0), stop=(i == n - 1)),
                         r=[("PT", bi), ("VA", kt)], w=[("bank", 4 + h // 4)])
                if h % 4 == 3:
                    hb2 = h // 4
                    P.act(I("activation", YO[:, 260 * hb2:260 * hb2 + 260], ob[:, 0:260], AF.Copy), r=[("bank", 4 + hb2)], w=[("YO", hb2)])

            def emit_post(jpl):
                jp = 8 * hh + jpl
                for hb2 in range(2):
                    o3 = YO[:, 260 * hb2:260 * hb2 + 260].rearrange("p (h c) -> p h c", c=65)
                    P.dve(I("reciprocal", RD[:, 4 * hb2:4 * hb2 + 4], o3[:, :, 64]), r=[("YO", hb2)], w=[("RD", hb2)])
                    P.dve(I("tensor_tensor", YA[:, 256 * hb2:256 * hb2 + 256].rearrange("p (h c) -> p h c", c=64), o3[:, :, 0:64],
                            RD[:, 4 * hb2:4 * hb2 + 4].unsqueeze(2).to_broadcast([128, 4, 64]), ALU.mult),
                          r=[("YO", hb2), ("RD", hb2)], w=[("YA", hb2)])
                c = newcol()
                P.act(I("activation", scr[:, 0:512], YA[:], AF.Square), r=[("YA", 0), ("YA", 1)], w=["scr"])
                P.dve(I("reduce_sum", colt[:, c:c + 1], scr[:, 0:512], axis=AX.X), r=["scr"], w=[("col", c)])
                rstd_from_ss(colt[:, c:c + 1], 512.0, ("col", c), ("col", c))
                P.dve(I("scalar_tensor_tensor", MA[:], YA[:], colt[:, c:c + 1], ZAG[:, jpl, :], ALU.mult, ALU.mult),
                      r=[("YA", 0), ("YA", 1), ("col", c), ("ZAG", jpl)], w=["MA"])
                bk = banks[6 + (jpl % 2)]
                bkb = bk[:].bitcast(BF16)
                for kc in range(4):
                    P.pe(I("transpose", bkb[:, kc * 128:(kc + 1) * 128], MA[:, kc * 128:(kc + 1) * 128], ident[:]), r=["MA", "ident"], w=[("bank", 6 + (jpl % 2))])
                P.act(I("activation", MAT[:, :, jp * 128:(jp + 1) * 128], bkb[:, 0:512].rearrange("p (k c) -> p k c", c=128), AF.Copy),
                      r=[("bank", 6 + (jpl % 2))], w=[("MAT", jp)])

            emit_S(0)
            for ui in range(len(units)):
                if ui + 1 < len(units):
                    emit_S(ui + 1)
                emit_softmax(ui)
                emit_PV(ui)
                if units[ui][1] == 7:
                    emit_post(units[ui][0])

        _o = [16384]
        def carve(n, dt=BF16):
            k = n if dt == BF16 else 2 * n
            v = R_B[:, _o[0]:_o[0] + k]
            _o[0] += k
            return v if dt == BF16 else v.bitcast(F32)
        CT = [carve(512) for i in range(4)]
        SN = [carve(1024) for i in range(4)]
        BU = [carve(1024) for i in range(2)]
        GI2 = [carve(1024) for i in range(2)]; GG2 = [carve(1024) for i in range(2)]
        TA2 = [carve(1024) for i in range(2)]
        HH = [carve(1024) for i in range(2)]
        assert _o[0] <= 32768, _o[0]
        ANG = scr[:, 0:512]; ANG3 = scr[:, 512:1024]
        v3 = lambda ap: ap.rearrange("p (a c) -> p a c", a=2)
        swp = lambda ap: v3(ap)[:, ::-1, :]
        tabctr = [0]

        def tables_gen(cols, par):
            for c in range(2):
                idx = cols[c]
                ti = par * 2 + c
                P.act(I("activation", ANG[:], iota1[:], AF.Copy, scale=TH[:, idx:idx + 1]), r=["iota1", "TH"], w=["scr"])
                yield
                for which in (1, 0):
                    if which == 0:
                        P.act(I("activation", ANG[:], ANG[:], AF.Identity, bias=cHP[:, 0:1], scale=1.0), r=["scr", "cM"], w=["scr"])
                        yield
                    P.act(I("activation", ANG3[:], ANG[:], AF.Identity, bias=cM0[:, 0:1], scale=1.0 / TWO_PI), r=["scr", "cM"], w=["scr"])
                    yield
                    P.act(I("activation", ANG3[:], ANG3[:], AF.Identity, bias=cMn[:, 0:1], scale=1.0), r=["scr", "cM"], w=["scr"])
                    yield
                    P.dve(I("scalar_tensor_tensor", ANG3[:], ANG3[:], -TWO_PI, ANG[:], ALU.mult, ALU.add), r=["scr"], w=["scr"])
                    yield
                    P.dve(I("tensor_scalar", ANG3[:], ANG3[:], -3.14159, 3.14159, ALU.max, ALU.min), r=["scr"], w=["scr"])
                    yield
                    if which == 1:
                        P.act(I("activation", SN[ti][:, 0:512], ANG3[:], AF.Sin, scale=0.999996), r=["scr"], w=[("SN", ti)])
                        yield
                        P.act(I("activation", SN[ti][:, 512:1024], ANG3[:], AF.Sin, scale=-0.999996), r=["scr"], w=[("SN", ti)])
                        yield
                    else:
                        P.act(I("activation", CT[ti][:], ANG3[:], AF.Sin, scale=0.999996), r=["scr"], w=[("CT", ti)])
                        yield

        def run_groups(groups):
            for n, g in enumerate(groups):
                if g.get("pre") is not None:
                    g["pre"]()
                if n == 0:
                    for _ in tables_gen(g["cols"], 0):
                        pass
                gens = g["chains"](n % 2)
                if g.get("side") is not None:
                    gens.append(g["side"]())
                if n + 1 < len(groups):
                    gens.append(tables_gen(groups[n + 1]["cols"], (n + 1) % 2))
                run_chains(gens)
                if g.get("post") is not None:
                    g["post"]()

        def scan_chain(c, rev, col, bm, bmkey, ub, ukey, kcp, par, mode, init0=None, cm=None, ystate=None, gcol=None):
            ti = par * 2 + c
            CC = CT[ti][:].unsqueeze(1).to_broadcast([128, 2, 512])
            tk = [("CT", ti), ("SN", ti)]
            TA, GI, GG, H2, BUd = TA2[c], GI2[c], GG2[c], HH[c], BU[c]
            rb = RC[:, col:col + 1].to_broadcast([128, 512])
            for kk in range(4):
                k = kk if not rev else 3 - kk
                for ri in range(2):
                    bk = banks[4 + 2 * c + ri]
                    P.pe(I("matmul", bk[:], bm(ri), ub[:, kcp, 512 * k:512 * k + 512], start=True, stop=True),
                         r=[bmkey, (ukey, kcp, k)], w=[("bank", 4 + 2 * c + ri)])
                    src = bk[:] if not rev else bk[:, ::-1]
                    P.act(I("activation", BUd[:, ri * 512:(ri + 1) * 512], src, AF.Copy), r=[("bank", 4 + 2 * c + ri)], w=[("BU", c)])
                yield
                P.dve(I("tensor_tensor", v3(TA), v3(BUd), CC, ALU.mult), r=[("BU", c)] + tk, w=[("TA", c)])
                yield
                P.dve(I("tensor_tensor", v3(GI), swp(BUd), v3(SN[ti]), ALU.mult), r=[("BU", c)] + tk, w=[("GI", c)])
                yield
                P.dve(I("tensor_tensor", GI[:], TA[:], GI[:], ALU.add), r=[("TA", c), ("GI", c)], w=[("GI", c)])
                yield
                for ri in range(2):
                    if mode == "own":
                        if kk == 0:
                            init, ir = init0(ri)
                        else:
                            c0 = ri * 512 + (511 if not rev else 0)
                            init, ir = H2[:, c0:c0 + 1], [("HH", c)]
                    else:
                        init, ir = HLc[:, ri, gcol:gcol + 1], [("HLc", gcol)]
                    P.dve(I("tensor_tensor_scan", GG[:, ri * 512:(ri + 1) * 512], rb, GI[:, ri * 512:(ri + 1) * 512], init, ALU.mult, ALU.add),
                          r=[("GI", c), "RC"] + ir, w=[("GG", c)])
                    yield
                if mode == "own":
                    hout = v3(H2) if not rev else v3(H2)[:, :, ::-1]
                    P.dve(I("tensor_tensor", v3(TA), v3(GG), CC, ALU.mult), r=[("GG", c)] + tk, w=[("TA", c)])
                    yield
                    P.dve(I("tensor_tensor", v3(GI), swp(GG), swp(SN[ti]), ALU.mult), r=[("GG", c)] + tk, w=[("GI", c)])
                    yield
                    P.dve(I("tensor_tensor", hout, v3(TA), v3(GI), ALU.add), r=[("TA", c), ("GI", c)], w=[("HH", c)])
                    yield
                    for ri in range(2):
                        ystate[k] += 1
                        P.pe(I("matmul", banks[k][:], cm(ri), H2[:, ri * 512:(ri + 1) * 512], start=(ystate[k] == 1), stop=(ystate[k] == 16)),
                             r=["CmT", ("HH", c)], w=[("bank", k)])
                    yield
                else:
                    t1, t2 = (TL1, TL2) if c == 0 else (TL3, TL4)
                    P.dve(I("tensor_tensor", t1[:].unsqueeze(2), v3(GG)[:, :, 511:512], CT[ti][:, 511:512].unsqueeze(1).to_broadcast([128, 2, 1]), ALU.mult),
                          r=[("GG", c)] + tk, w=[("TL", c, 0)])
                    yield
                    P.dve(I("tensor_tensor", t2[:].unsqueeze(2), swp(GG)[:, :, 511:512], swp(SN[ti])[:, :, 511:512], ALU.mult), r=[("GG", c)] + tk, w=[("TL", c, 1)])
                    yield
                    P.dve(I("tensor_tensor", HLc[:, :, gcol:gcol + 1], t1[:].unsqueeze(2), t2[:].unsqueeze(2), ALU.add), r=[("TL", c, 0), ("TL", c, 1)], w=[("HLc", gcol)])
                    yield

        def run_chains(gens):
            gens = list(gens)
            while gens:
                for g in list(gens):
                    try:
                        next(g)
                    except StopIteration:
                        gens.remove(g)

        def ssm_chunk(ci, correction):
            groups = []
            for kcp in range(4):
                ystate = {k: 0 for k in range(4)}
                for gq in range(4):
                    gp = 4 * kcp + gq
                    def chains(par, gp=gp, kcp=kcp, ystate=ystate):
                        out = []
                        for d in range(2):
                            idx = d * 16 + gp
                            if ci == 1:
                                init0 = (lambda ri, idx=idx: (HFB[:, ri, idx:idx + 1], ["HFB"]))
                            else:
                                init0 = (lambda ri: (0.0, []))
                            out.append(scan_chain(d, d == 1, idx,
                                                  (lambda ri, d=d, gp=gp: BmT[:, (d * 2 + ri) * 16 + gp, :]), "BmT", uT, "uT", kcp, par, "own",
                                                  init0=init0, cm=(lambda ri, d=d, gp=gp: CmT[:, (d * 2 + ri) * 16 + gp, :]), ystate=ystate))
                        return out
                    post = None
                    if gq == 3:
                        def post(kcp=kcp):
                            for k in range(4):
                                P.dve(I("scalar_tensor_tensor", Y0T[:, kcp, 512 * k:512 * k + 512], uT[:, kcp, 512 * k:512 * k + 512], dskc[:, kcp:kcp + 1], banks[k][:], ALU.mult, ALU.add),
                                      r=[("bank", k), ("uT", kcp, k), "dskc"], w=[("Y0T", kcp, k)] + [("hT", t_) for t_ in range(16)])
                    groups.append(dict(cols=(gp, 16 + gp), chains=chains, post=post))
            run_groups(groups)

        def carry_scan():
            toks = [(512 * k, 512, 512 * k) for k in range(4)]
            hl_all = [("HLc", g) for g in range(16)]
            P.dve(I("memset", HLc[:], 0.0), w=hl_all)
            P.dve(I("memset", HFB[:], 0.0), w=["HFB"])
            groups = []
            UB = [(uT, "uT"), (ZSG, "ZSG")]
            def proj_gen(sl):
                ub, ukey = UB[sl % 2]
                P.dma(I("dma_start", out=gvec[:], in_=ng_d), w=["gvec"])
                for t in range(16):
                    norm_tiles(1, 2048 * sl + 128 * t, 1, src=xoth, t0=t, bankbase=0)
                    yield
                wi = load_wgroup(0)
                cnt = 0
                for cc in range(4):
                    for (h0, n, dst0) in toks:
                        b = 2 + cnt % 2; cnt += 1
                        bk = banks[b]
                        for kc in range(8):
                            P.pe(I("matmul", bk[:, 0:n], WS[wi][:, kc, cc * 128:(cc + 1) * 128], hT[:, kc, h0:h0 + n], start=(kc == 0), stop=(kc == 7)),
                                 r=[("WS", wi)] + [("hT", t) for t in range(h0 // 128, (h0 + n) // 128)], w=[("bank", b)])
                        P.act(I("activation", ub[:, cc, dst0:dst0 + n], bk[:, 0:n], AF.Copy), r=[("bank", b)], w=[(ukey, cc, dst0 // 512)])
                        yield
            for _ in proj_gen(0):
                pass
            for sl in range(3):
                def post(sl=sl):
                    for dsel in range(2):
                        P.dve(I("scalar_tensor_tensor", HFB[:, :, dsel * 16:dsel * 16 + 16], HLc[:], LSEL[:, 3 + dsel * 3 + sl:4 + dsel * 3 + sl],
                                HFB[:, :, dsel * 16:dsel * 16 + 16], ALU.mult, ALU.add), r=hl_all + ["LSEL", "HFB"], w=["HFB"])
                    if sl < 2:
                        P.dve(I("tensor_scalar", HLc[:], HLc[:], LSEL[:, sl + 1:sl + 2], None, ALU.mult), r=hl_all + ["LSEL"], w=hl_all)
                for g2 in range(8):
                    gps = (2 * g2, 2 * g2 + 1)
                    def chains(par, gps=gps, sl=sl):
                        out = []
                        for c in range(2):
                            gp = gps[c]
                            out.append(scan_chain(c, False, 32 + sl * 16 + gp,
                                                  (lambda ri, gp=gp, sl=sl: BmTc[:, (sl * 2 + ri) * 16 + gp, :]), "BmTc", UB[sl % 2][0], UB[sl % 2][1], gp // 4, par, "carry", gcol=gp))
                        return out
                    groups.append(dict(cols=(32 + sl * 16 + gps[0], 32 + sl * 16 + gps[1]), chains=chains,
                                       side=((lambda sl=sl: proj_gen(sl + 1)) if (g2 == 0 and sl < 2) else None), post=(post if g2 == 7 else None)))
            run_groups(groups)

        def ssm_project(ci):
            P.dma(I("dma_start", out=gvec[:], in_=ng_d), w=["gvec"])
            norm_tiles(ci, OWN0, 16)
            toks = [(512 * k, 512, 512 * k) for k in range(4)]
            def ev_u(cc, bk, b, n, dst0):
                P.act(I("activation", uT[:, cc, dst0:dst0 + n], bk[:, 0:n], AF.Copy), r=[("bank", b)], w=[("uT", cc, dst0 // 512)])
            fm_proj(0, toks, ev_u)
            def ev_zs(cc, bk, b, n, dst0):
                i = sgctr[0] % 2; sgctr[0] += 1
                P.act(I("activation", sigt[i][:, 0:n], bk[:, 0:n], AF.Sigmoid), r=[("bank", b)], w=[("sigt", i)])
                P.dve(I("scalar_tensor_tensor", ZSG[:, cc, dst0:dst0 + n], bk[:, 0:n], sgc[:, cc:cc + 1], sigt[i][:, 0:n], ALU.mult, ALU.mult),
                      r=[("bank", b), ("sigt", i), "sgc"], w=[("ZSG", cc, dst0 // 512)])
            fm_proj(1, toks, ev_zs)

        XA = scr[:, 0:512]; XB = scr[:, 512:1024]

        def ssm_post(ci):
            for kcp in range(4):
                for k in range(4):
                    y = Y0T[:, kcp, 512 * k:512 * k + 512]
                    yk = ("Y0T", kcp, k)
                    gi_ = (kcp * 4 + k) % 2
                    xa, xb = (XA, XB) if gi_ == 0 else (stg[0][:, 0:512], stg[0][:, 512:1024])
                    ka, kb = ("XA", gi_), ("XB", gi_)
                    P.dve(I("tensor_tensor", xa, y, y, ALU.mult), r=[yk], w=[ka])
                    P.dve(I("tensor_scalar", xa, xa, 0.044715, 1.0, ALU.mult, ALU.add), r=[ka], w=[ka])
                    P.dve(I("tensor_tensor", xa, xa, y, ALU.mult), r=[ka, yk], w=[ka])
                    P.act(I("activation", xb, xa, AF.Sigmoid, scale=1.5957691216057308), r=[ka], w=[kb])
                    P.dve(I("tensor_tensor", YGT[:, kcp, 512 * k:512 * k + 512], y, xb, ALU.mult), r=[kb, yk], w=[("YGT", kcp, k)])
            for k in range(4):
                for kc2 in range(4):
                    b = pbank[0] % 4; pbank[0] += 1
                    bk = banks[b]
                    for kcp in range(4):
                        P.pe(I("matmul", bk[:], WGB[:, kcp, kc2 * 128:(kc2 + 1) * 128], YGT[:, kcp, 512 * k:512 * k + 512], start=(kcp == 0), stop=(kcp == 3)),
                             r=["WGB"] + [("YGT", q, k) for q in range(4)], w=[("bank", b)])
                    i = sgctr[0] % 2; sgctr[0] += 1
                    P.act(I("activation", sigt[i][:], bk[:], AF.Sigmoid, bias=bgc[:, kc2:kc2 + 1]), r=[("bank", b), "bgc"], w=[("sigt", i)])
                    ysl = YGT[:, kc2, 512 * k:512 * k + 512]
                    P.dve(I("tensor_tensor", sigt[i][:], sigt[i][:], ysl, ALU.mult), r=[("sigt", i), ("YGT", kc2, k)], w=[("sigt", i)])
                    P.dve(I("tensor_tensor", Y2Q[:, kc2, 512 * k:512 * k + 512], sigt[i][:], sigt[i][:], ALU.mult), r=[("sigt", i)], w=[("Y2Q", kc2, k)])
                    P.dve(I("tensor_tensor", ZSG[:, kc2, 512 * k:512 * k + 512], ZSG[:, kc2, 512 * k:512 * k + 512], sigt[i][:], ALU.mult),
                          r=[("sigt", i), ("ZSG", kc2, k)], w=[("ZSG", kc2, k)])
            for tt in range(16):
                bk = banks[6 + tt % 2]
                for kc2 in range(4):
                    P.pe(I("matmul", bk[:, 0:1], Y2Q[:, kc2, tt * 128:(tt + 1) * 128], ones_c[:], start=(kc2 == 0), stop=(kc2 == 3)),
                         r=[("Y2Q", kc2, tt // 4), "ones_c"], w=[("bank", 6 + tt % 2)])
                P.act(I("activation", RSTD_S[:, tt:tt + 1], bk[:, 0:1], AF.Sqrt, bias=epsc[:, 0:1], scale=1.0 / 512), r=[("bank", 6 + tt % 2), "epsc"], w=[("RS", tt)])
                P.dve(I("reciprocal", RSTD_S[:, tt:tt + 1], RSTD_S[:, tt:tt + 1]), r=[("RS", tt)], w=[("RS", tt)])

        OT = [R_A[:, 1024 * i:1024 * (i + 1)] for i in range(2)]

        def out_proj(ci):
            P.dma(I("dma_start", out=gvec[:], in_=fg_d), w=["gvec"])
            wis = []
            for half in range(2):
                i = wsctr[0] % 2; wsctr[0] += 1
                P.dma(I("dma_start", out=WS[i][:], in_=woutb.ap()[:, half * 512:(half + 1) * 512].rearrange("(k p) c -> p k c", p=128)), r=["woutb"], w=[("WS", i)])
                wis.append(i)
            for tt in range(16):
                i = tilectr[0] % 2; tilectr[0] += 1
                P.dma(I("dma_start", out=stg[i][:], in_=xc[ci, OWN0 + tt * 128:OWN0 + (tt + 1) * 128, :]), w=[("stg", i)])
                acc = scr if tt % 2 == 0 else R_A[:, 2048:3072]
                ak = tt % 2
                for half in range(2):
                    bs, ba = banks[half], banks[2 + half]
                    for kc in range(4):
                        P.pe(I("matmul", bs[:], ZSG[:, kc, tt * 128:(tt + 1) * 128], WS[wis[half]][:, kc, :], start=(kc == 0), stop=(kc == 3)),
                             r=[("ZSG", kc, tt // 4), ("WS", wis[half])], w=[("bank", half)])
                    for kc in range(4):
                        P.pe(I("matmul", ba[:], MAT[:, kc, tt * 128:(tt + 1) * 128], WS[wis[half]][:, 4 + kc, :], start=(kc == 0), stop=(kc == 3)),
                             r=[("MAT", tt), ("WS", wis[half])], w=[("bank", 2 + half)])
                    hs = slice(512 * half, 512 * half + 512)
                    P.dve(I("scalar_tensor_tensor", acc[:, hs], bs[:], RSTD_S[:, tt:tt + 1], stg[i][:, hs], ALU.mult, ALU.add),
                          r=[("bank", half), ("RS", tt), ("stg", i)], w=[("scrh", ak, half)])
                    P.dve(I("tensor_tensor", acc[:, hs], acc[:, hs], ba[:], ALU.add), r=[("bank", 2 + half), ("scrh", ak, half)], w=[("scrh", ak, half)])
                c = newcol()
                oi = tt % 2
                P.act(I("activation", OT[oi][:], acc[:], AF.Square), r=[("scrh", ak, 0), ("scrh", ak, 1)], w=[("OT", oi)])
                P.dve(I("reduce_sum", colt[:, c:c + 1], OT[oi][:], axis=AX.X), r=[("OT", oi)], w=[("col", c)])
                rstd_from_ss(colt[:, c:c + 1], 1024.0, ("col", c), ("col", c))
                P.dve(I("scalar_tensor_tensor", OT[oi][:], acc[:], colt[:, c:c + 1], gvec[:], ALU.mult, ALU.mult),
                      r=[("scrh", ak, 0), ("scrh", ak, 1), ("col", c), "gvec"], w=[("OT", oi)])
                P.dma(I("dma_start", out=yo[ci, tt * 128:(tt + 1) * 128, :], in_=OT[oi][:]), r=[("OT", oi)], w=[("yo", ci, tt)])

        SR = scr[:, 0:512].rearrange("p (r c) -> p r c", c=64); TW = scr[:, 512:1024].rearrange("p (r c) -> p r c", c=64); TK = SB("TK", [128, 3, 64])
        HIN = SB("HIN", [128, 64])

        def carry_exchange():
            ld(WSEL.rearrange("p k r c -> p (k r c)"), wsel_d.rearrange("p (k r c) -> p k r c", k=3, r=8).rearrange("p k r c -> p (k r c)"), "WSEL")
            P.dma(I("dma_start", out=ccin.ap(), in_=SPK[:]), r=[("SPK", c) for c in range(64)], w=["ccin"], q="pool")
            def cc(e):
                ins = e.collective_compute("AllGather", ALU.bypass, replica_groups=[list(range(8))], ins=[ccin.ap().opt()], outs=[ccout.ap().opt()])
                ins.then_inc(cc_sem)
                e.wait_ge(cc_sem, 1)
                return e.nop()
            P.pool(cc, r=["ccin"], w=["ccout"])
            P.dma(I("dma_start", out=SR[:], in_=ccout.ap().rearrange("(r p) c -> p r c", p=128)), r=["ccout"], w=["SR"], q="pool")
            for k in range(3):
                P.dve(I("tensor_tensor", TW[:], SR[:], WSEL[:, k, :, :], ALU.mult), r=["SR", "WSEL"], w=["TW"])
                P.dve(I("reduce_sum", TK[:, k, :], TW[:].rearrange("p r c -> p c r"), axis=AX.X), r=["TW"], w=[("TK", k)])
            t_re = lambda k: TK[:, k, 0:32]
            t_im = lambda k: TK[:, k, 32:64]
            P.dve(I("tensor_copy", HIN[:], TK[:, 0, :]), r=[("TK", 0)], w=["HIN"])
            for k, pi_ in ((1, 3), (2, 4)):
                ar, ai = PWR[:, pi_, :], PWI[:, pi_, :]
                P.dve(I("tensor_tensor", tmpa[:], t_re(k), ar, ALU.mult), r=[("TK", k), "PW%dr" % pi_], w=["tmpa"])
                P.dve(I("tensor_tensor", tmpb[:], t_im(k), ai, ALU.mult), r=[("TK", k), "PW%di" % pi_], w=["tmpb"])
                P.dve(I("tensor_tensor", tmpa[:], tmpa[:], tmpb[:], ALU.subtract), r=["tmpa", "tmpb"], w=["tmpa"])
                P.dve(I("tensor_tensor", HIN[:, 0:32], HIN[:, 0:32], tmpa[:], ALU.add), r=["tmpa", "HIN"], w=["HIN"])
                P.dve(I("tensor_tensor", tmpa[:], t_re(k), ai, ALU.mult), r=[("TK", k), "HIN"], w=["tmpa"])
                P.dve(I("tensor_tensor", tmpb[:], t_im(k), ar, ALU.mult), r=[("TK", k), "HIN"], w=["tmpb"])
                P.dve(I("tensor_tensor", tmpa[:], tmpa[:], tmpb[:], ALU.add), r=["tmpa", "tmpb"], w=["tmpa"])
                P.dve(I("tensor_tensor", HIN[:, 32:64], HIN[:, 32:64], tmpa[:], ALU.add), r=["tmpa", "HIN"], w=["HIN"])
            P.dve(I("tensor_copy", HKc[:, 0, :, 0], HIN[:, 0:32]), r=["HIN"], w=["HK"])
            P.dve(I("tensor_copy", HKc[:, 0, :, 1], HIN[:, 32:64]), r=["HIN"], w=["HK"])
            for kk in range(1, 4):
                ar, ai = PWR[:, kk - 1, :], PWI[:, kk - 1, :]
                P.dve(I("tensor_tensor", tmpa[:], HIN[:, 0:32], ar, ALU.mult), r=["HIN", "HK"], w=["tmpa"])
                P.dve(I("tensor_tensor", tmpb[:], HIN[:, 32:64], ai, ALU.mult), r=["HIN", "HK"], w=["tmpb"])
                P.dve(I("tensor_tensor", HKc[:, kk, :, 0], tmpa[:], tmpb[:], ALU.subtract), r=["tmpa", "tmpb"], w=["HK"])
                P.dve(I("tensor_tensor", tmpa[:], HIN[:, 0:32], ai, ALU.mult), r=["HIN", "HK"], w=["tmpa"])
                P.dve(I("tensor_tensor", tmpb[:], HIN[:, 32:64], ar, ALU.mult), r=["HIN", "HK"], w=["tmpb"])
                P.dve(I("tensor_tensor", HKc[:, kk, :, 1], tmpa[:], tmpb[:], ALU.add), r=["tmpa", "tmpb"], w=["HK"])

        for ci in (1, 0):
            P.barrier()
            build_masks()
            P.barrier()
            attention_half(ci, 0)
            attention_half(ci, 1)
            P.barrier()
            if ci == 1:
                build_bc_carry()
                P.barrier()
                carry_scan()
                P.barrier()
            build_bc()
            P.barrier()
            ssm_project(ci)
            ssm_chunk(ci, correction=False)
            P.barrier()
            ssm_post(ci)
            P.barrier()
            out_proj(ci)
        P.emit()
    return nc


_NC_CACHE = {}


def _host_inputs(c, inp):
    f32 = np.float32
    bf = ml_dtypes.bfloat16
    q, sq = c % 4, c // 4
    xc = np.zeros((2, NBUF, 1024), f32)
    xc[0, OWN0:OWN0 + NOWN] = inp["x_prompt"][c]
    g0 = 2048 * q - 256
    lo, hi = max(g0, 0), min(g0 + NBUF, 8192)
    xc[1, lo - g0:hi - g0] = inp["x_sample"][sq, lo:hi]
    rep = lambda v, n=128: np.ascontiguousarray(np.broadcast_to(np.asarray(v, f32).reshape(1, -1), (n, np.asarray(v).size)))
    colm = lambda v: np.ascontiguousarray(np.asarray(v, f32).reshape(4, 128).T)
    def pl(a):
        a = np.asarray(a, f32).reshape(2, 16, 2, 64)
        return np.ascontiguousarray(a.transpose(2, 3, 0, 1).reshape(128, 32))
    def plb(a):
        a = np.asarray(a, f32).reshape(2, 16, 2, 64, 16)
        return np.ascontiguousarray(a.transpose(2, 3, 0, 1, 4).reshape(128, 512))
    def plc(a):
        a = np.asarray(a, f32).reshape(2, 16, 2, 16, 64)
        return np.ascontiguousarray(a.transpose(2, 4, 0, 1, 3).reshape(128, 512))
    logdt = np.broadcast_to(np.asarray(inp["log_dt"][0], f32)[:, :, None], (2, 32, 64))
    rpbp = np.zeros((8, 16, 127), f32)
    rpbp[:, :15, 48:79] = inp["rpb"][0]
    kc = np.arange(128) % 64
    qc = np.arange(64)
    qs = np.clip(qc - 8, 0, 48)
    cv = ((kc[:, None] >= qs[None, :]) & (kc[:, None] < qs[None, :] + 16)).astype(f32)
    a_ = (np.arange(128) // 64)
    rvi = np.zeros((128, 5, 2), f32)
    for oi, o in enumerate((-4, -2, 0, 2, 4)):
        for b in range(2):
            ro = o + a_ - b
            rvi[:, oi, b] = ((ro >= -4) & (ro <= 3))
    rvc = np.zeros((128, 2, 4, 7, 2), f32)
    for ci in range(2):
        R0, rows = (0, 32) if ci == 0 else (32 * q, 128)
        for jpb, jp in enumerate((0, 1, 14, 15)):
            for oi in range(7):
                o = 2 * oi - 6
                for b in range(2):
                    j = 2 * jp + b
                    ls = np.clip(j + R0 - 4, 0, rows - 8) - R0
                    kr = 2 * jp + o + a_
                    ok = (kr >= ls) & (kr < ls + 8) & (kr + R0 >= 0) & (kr + R0 < rows)
                    rvc[:, ci, jpb, oi, b] = ok
    kex = np.zeros((128, 2, 20), f32)
    kex[:, 0, 2:18] = 1.0
    for t in range(20):
        g = g0 + 128 * t
        kex[:, 1, t] = 1.0 if (0 <= g < 8192) else 0.0
    wsel = np.zeros((128, 3, 8, 64), f32)
    for r in range(8):
        if r // 4 != sq:
            continue
        qp = r % 4
        for col in range(64):
            d = (col % 32) // 16
            if d == 0 and qp < q:
                wsel[:, q - 1 - qp, r, col] = 1.0
            if d == 1 and qp > q:
                wsel[:, qp - q - 1, r, col] = 1.0
    iota1 = np.ascontiguousarray(np.broadcast_to(np.arange(1, 513, dtype=f32)[None, :], (128, 512)))
    slots = [(qq, 0) for qq in range(q)] + [(qq, 1) for qq in range(3, q, -1)]
    parts = []
    for qq, dd in slots:
        xs = inp["x_sample"][sq, 2048 * qq:2048 * qq + 2048]
        parts.append(xs[::-1] if dd == 1 else xs)
    xoth = np.ascontiguousarray(np.concatenate(parts, 0), dtype=f32)
    lsel = np.zeros((128, 9), f32)
    for si in range(3):
        dd = slots[si][1]
        if si > 0 and slots[si - 1][1] == dd:
            lsel[:, si] = 1.0
        last_of_chain = (si == 2) or (slots[si + 1][1] != dd)
        if last_of_chain:
            lsel[:, 3 + dd * 3 + si] = 1.0
    def widen(a32):
        return np.ascontiguousarray(np.concatenate([a32] + [a32[:, dd * 16:(dd + 1) * 16] for _, dd in slots], 1))
    def bsel(a512):
        a3 = a512.reshape(128, 32, 16)
        return np.ascontiguousarray(np.concatenate([a3[:, dd * 16:(dd + 1) * 16] for _, dd in slots], 1).reshape(128, 768))
    return {
        "xc": xc, "w_in": np.ascontiguousarray(inp["w_in"][0], f32), "w_out": np.ascontiguousarray(inp["w_out"][0], f32),
        "w_glu": np.ascontiguousarray(inp["w_glu"][0], f32),
        "ng": rep(inp["norm_g"][0]), "fg": rep(inp["final_norm_g"]), "ag": rep(inp["attn_out_g"][0]),
        "sgc": colm(inp["ssm_out_g"][0]), "bgc": colm(inp["b_glu"][0]), "dskc": colm(inp["d_skip"][0]),
        "lamr": widen(pl(inp["lam_re"][0])), "lami": widen(pl(inp["lam_im"][0])), "logdt": widen(pl(logdt)),
        "brec": bsel(plb(inp["b_re"][0])), "bimc": bsel(plb(inp["b_im"][0])), "lsel": lsel,
        "bre": plb(inp["b_re"][0]), "bim": plb(inp["b_im"][0]), "cre": plc(inp["c_re"][0]), "cim": plc(inp["c_im"][0]),
        "rpbp": rpbp, "cv": cv.astype(bf), "rvi": rvi.reshape(128, 10).astype(bf), "rvc": rvc.reshape(128, 112).astype(bf),
        "xoth": xoth, "kex": kex.reshape(128, 40), "iota1": iota1,
    }


def kernel(**inputs):
    inp = {k: np.asarray(v) for k, v in inputs.items()}
    if "nc" not in _NC_CACHE:
        _NC_CACHE["nc"] = build_program()
    nc = _NC_CACHE["nc"]
    in_maps = [_host_inputs(c, inp) for c in range(8)]
    res = run_bass_kernel_spmd(nc, in_maps, core_ids=list(range(8)))
    y_prompt = np.zeros((8, 2048, 1024), np.float32)
    y_sample = np.zeros((2, 8192, 1024), np.float32)
    for c in range(8):
        yo = np.asarray(res.results[c]["yo"], dtype=np.float32)
        y_prompt[c] = yo[0]
        y_sample[c // 4, 2048 * (c % 4):2048 * (c % 4) + 2048] = yo[1]
    return (y_prompt, y_sample)
```

```python
import contextlib
import numpy as np
import ml_dtypes
import concourse.bass as bass
import concourse.mybir as mybir
from concourse.bass_utils import run_bass_kernel_spmd

F32 = mybir.dt.float32
BF16 = mybir.dt.bfloat16
ALU = mybir.AluOpType
AF = mybir.ActivationFunctionType
AX = mybir.AxisListType

COMPUTE = ("pe", "act", "dve", "pool")


class Prog:
    def __init__(self, nc, n_dma_sems=(20, 12)):
        self.nc = nc
        self.ops = []
        self.streams = {e: [] for e in ("pe", "act", "dve", "pool", "sp")}
        self.last_w = {}
        self.readers = {}
        self.n_dma_sems = {"sp": n_dma_sems[0], "pool": n_dma_sems[1], "act": 0}
        self.dma_count = {"sp": 0, "pool": 0}
        self.dma_last_on_sem = {}
        self.barrier_deps = set()
        self.since_barrier = set()

    def add(self, eng, fn, reads=(), writes=(), dma=False, extra_deps=()):
        oid = len(self.ops)
        op = dict(id=oid, eng=eng, fn=fn, dma=dma, deps=set(extra_deps), signals=dma, token=None)
        deps = op["deps"]
        deps |= self.barrier_deps
        for k in reads:
            w = self.last_w.get(k)
            if w is not None:
                wo = self.ops[w]
                if not (eng == "pe" and wo["eng"] == "pe" and not dma):
                    deps.add(w)
        for k in writes:
            w = self.last_w.get(k)
            if w is not None:
                wo = self.ops[w]
                if not (eng == "pe" and wo["eng"] == "pe" and not dma and not wo["dma"]):
                    deps.add(w)
            for r in self.readers.get(k, ()):
                ro = self.ops[r]
                if not (eng == "pe" and ro["eng"] == "pe" and not dma and not ro["dma"]):
                    deps.add(r)
        for k in reads:
            lst = self.readers.setdefault(k, [])
            if not dma:
                lst[:] = [r for r in lst if self.ops[r]["dma"] or self.ops[r]["eng"] != eng]
            lst.append(oid)
        for k in writes:
            self.last_w[k] = oid
            self.readers[k] = []
        if dma:
            n = self.dma_count[eng]
            self.dma_count[eng] = n + 1
            slot = (eng, n % self.n_dma_sems[eng])
            prev = self.dma_last_on_sem.get(slot)
            if prev is not None:
                deps.add(prev)
                val = self.ops[prev]["token"][1] + 16
            else:
                val = 16
            op["token"] = (slot, val)
            self.dma_last_on_sem[slot] = oid
        deps.discard(oid)
        self.ops.append(op)
        self.streams[eng].append(oid)
        if dma:
            self.since_barrier.add(oid)
        return oid

    def barrier(self):
        deps = set(self.since_barrier)
        for e, s in self.streams.items():
            if s:
                deps.add(s[-1])
        for e in self.streams:
            for oid in reversed(self.streams[e]):
                if not self.ops[oid]["dma"]:
                    deps.add(oid)
                    break
        self.barrier_deps = deps
        self.since_barrier = set()

    def pe(self, fn, r=(), w=(), **kw):
        return self.add("pe", fn, r, w, **kw)

    def act(self, fn, r=(), w=(), **kw):
        return self.add("act", fn, r, w, **kw)

    def dve(self, fn, r=(), w=(), **kw):
        return self.add("dve", fn, r, w, **kw)

    def pool(self, fn, r=(), w=(), **kw):
        return self.add("pool", fn, r, w, **kw)

    def dma(self, fn, r=(), w=(), q="sp", **kw):
        return self.add(q, fn, r, w, dma=True, **kw)

    def emit(self):
        nc = self.nc
        ops = self.ops
        for op in ops:
            for d in op["deps"]:
                ops[d]["signals"] = True
        cnt = {e: 0 for e in COMPUTE}
        for op in ops:
            if not op["dma"] and op["signals"]:
                cnt[op["eng"]] += 1
                op["token"] = (op["eng"], cnt[op["eng"]])
        import contextlib
        with contextlib.ExitStack() as st:
            sems = {}
            for e in COMPUTE:
                sems[e] = st.enter_context(nc.semaphore("c_" + e))
            for q in ("sp", "pool"):
                for i in range(self.n_dma_sems[q]):
                    sems[(q, i)] = st.enter_context(nc.semaphore("d_%s_%d" % (q, i)))
            block = st.enter_context(nc.Block())
            streams = self.streams

            def run(eng_name, handle):
                waited = {}
                for oid in streams[eng_name]:
                    op = ops[oid]
                    need = {}
                    for d in op["deps"]:
                        s, v = ops[d]["token"]
                        if v > need.get(s, 0):
                            need[s] = v
                    for s, v in need.items():
                        if waited.get(s, 0) < v:
                            handle.wait_ge(sems[s], v)
                            waited[s] = v
                    f = op["fn"]
                    if isinstance(f, tuple):
                        ins = getattr(handle, f[0])(*f[1], **f[2])
                    else:
                        ins = f(handle)
                    if op["dma"]:
                        ins.then_inc(sems[op["token"][0]], 16)
                    elif op["signals"]:
                        ins.then_inc(sems[op["eng"]], 1)
                if eng_name in ("sp", "pool"):
                    for i in range(self.n_dma_sems[eng_name]):
                        last = self.dma_last_on_sem.get((eng_name, i))
                        if last is not None:
                            v = ops[last]["token"][1]
                            if waited.get((eng_name, i), 0) < v:
                                handle.wait_ge(sems[(eng_name, i)], v)

            @block.tensor
            def _(e):
                run("pe", e)

            @block.scalar
            def _(e):
                run("act", e)

            @block.vector
            def _(e):
                run("dve", e)

            @block.gpsimd
            def _(e):
                run("pool", e)

            @block.sync
            def _(e):
                run("sp", e)
        self.sem_counts = cnt


def I(name, *a, **k):
    return (name, a, k)


NBUF = 2560
OWN0 = 256
NOWN = 2048
EPS = 1e-6
TWO_PI = 6.283185307179586
MAGIC = 12582912.0


def build_program():
    nc = bass.Bass("TRN2", target_bir_lowering=False)
    DI = lambda name, shape, dt=F32: nc.dram_tensor(name, shape, dt, kind="ExternalInput").ap()
    xc = DI("xc", [2, NBUF, 1024])
    w_in = DI("w_in", [1024, 3072]); w_out = DI("w_out", [1024, 1024]); w_glu = DI("w_glu", [512, 512])
    ng_d = DI("ng", [128, 1024]); fg_d = DI("fg", [128, 1024]); ag_d = DI("ag", [128, 512])
    sgc_d = DI("sgc", [128, 4]); bgc_d = DI("bgc", [128, 4]); dskc_d = DI("dskc", [128, 4])
    lamr_d = DI("lamr", [128, 80]); lami_d = DI("lami", [128, 80]); logdt_d = DI("logdt", [128, 80])
    brec_d = DI("brec", [128, 768]); bimc_d = DI("bimc", [128, 768]); lsel_d = DI("lsel", [128, 9])
    bre_d = DI("bre", [128, 512]); bim_d = DI("bim", [128, 512]); cre_d = DI("cre", [128, 512]); cim_d = DI("cim", [128, 512])
    rpbp_d = DI("rpbp", [8, 16, 127])
    cv_d = DI("cv", [128, 64], BF16); rvi_d = DI("rvi", [128, 10], BF16); rvc_d = DI("rvc", [128, 2 * 4 * 14], BF16)
    kex_d = DI("kex", [128, 40])
    iota_d = DI("iota1", [128, 512])
    xoth = DI("xoth", [6144, 1024])
    yo = nc.dram_tensor("yo", [2, NOWN, 1024], F32, kind="ExternalOutput").ap()
    winb = nc.dram_tensor("winb", [1024, 3072], BF16)
    woutb = nc.dram_tensor("woutb", [1024, 1024], BF16)
    ccin = nc.dram_tensor("ccin", [128, 64], F32)
    ccout = nc.dram_tensor("ccout", [1024, 64], F32)

    with contextlib.ExitStack() as st:
        def SB(name, shape, dt=F32):
            return st.enter_context(nc.sbuf_tensor(name, shape, dt))
        banks = [st.enter_context(nc.psum_tensor("bank%d" % i, [128, 512], F32)) for i in range(8)]
        cc_sem = st.enter_context(nc.semaphore("ccs"))
        P = Prog(nc)

        ident = SB("ident", [128, 128], BF16)
        identf = SB("identf", [128, 128], F32)
        ones_c = SB("ones_c", [128, 1], BF16)
        epsc = SB("epsc", [128, 1], F32)
        cM0 = SB("cM0", [128, 1], F32); cM1 = SB("cM1", [128, 1], F32); cMn = SB("cMn", [128, 1], F32); cHP = SB("cHP", [128, 1], F32)
        iota1 = SB("iota1s", [128, 512], F32)
        sgc = SB("sgcs", [128, 4]); bgc = SB("bgcs", [128, 4]); dskc = SB("dskcs", [128, 4])
        kex = SB("kexs", [128, 40]); cv = SB("cvs", [128, 64], BF16)
        rvi = SB("rvis", [128, 10], BF16); rvc = SB("rvcs", [128, 112], BF16)
        ag = SB("ags", [128, 512]); gvec = SB("gvec", [128, 1024])
        WGB = SB("WGB", [128, 4, 512], BF16)
        TH = SB("TH", [128, 80]); LR = SB("LR", [128, 80]); RC = SB("RC", [128, 80])
        HKc = SB("HKc", [128, 4, 32, 2])
        SPK = SB("SPK", [128, 64])
        HLc = SB("HLc", [128, 2, 16]); HFB = SB("HFB", [128, 2, 32]); LSEL = SB("LSEL", [128, 9]); TL1 = SB("TL1", [128, 2]); TL2 = SB("TL2", [128, 2]); TL3 = SB("TL3", [128, 2]); TL4 = SB("TL4", [128, 2])
        R_BC = SB("R_BC", [128, 16384], BF16)
        BmT = R_BC[:, 0:8192].rearrange("p (m c) -> p m c", c=128)
        CmT = R_BC[:, 8192:16384].rearrange("p (m c) -> p m c", c=128)
        MB = R_BC[:, 0:7168].rearrange("p (h o c) -> p h o c", h=8, o=7)
        MBI = R_BC[:, 7168:12288].rearrange("p (h o c) -> p h o c", h=8, o=5)
        R_A = SB("R_A", [128, 8192], F32)
        hT = R_A[:].bitcast(BF16).rearrange("p (k t) -> p k t", k=8)
        Y0T = R_A[:].rearrange("p (k t) -> p k t", k=4)
        R_B = SB("R_B", [128, 32768], BF16)
        qT = R_B[:, 0:4096].rearrange("p (k t) -> p k t", k=4)
        kT = R_B[:, 4096:10240].rearrange("p (k t) -> p k t", k=4)
        VA = R_B[:, 10240:16480].rearrange("p (t h c) -> p t h c", t=12, h=8)
        ZAG = R_B[:, 16480:20576].rearrange("p (t c) -> p t c", t=8)
        RPE = R_B[:, 20576:28256].rearrange("p (h r c) -> p h r c", h=8, r=15)
        uT = R_B[:, 0:8192].rearrange("p (k t) -> p k t", k=4)
        ZSG = R_B[:, 8192:16384].rearrange("p (k t) -> p k t", k=4)
        YGT = R_B[:, 16384:24576].rearrange("p (k t) -> p k t", k=4)
        Y2Q = R_B[:, 24576:32768].rearrange("p (k t) -> p k t", k=4)
        MAT = SB("MAT", [128, 4, 2048], BF16)
        RSTD_S = SB("RSTD_S", [128, 16])
        WS = [SB("WS%d" % i, [128, 8, 512], BF16) for i in range(2)]
        stg = [SB("stg%d" % i, [128, 1024], F32) for i in range(2)]
        scr = SB("scr", [128, 1024], F32)
        hn = [SB("hn%d" % i, [128, 1024], BF16) for i in range(2)]
        colt = SB("colt", [128, 64], F32)
        colctr = [0]

        def newcol():
            colctr[0] = (colctr[0] + 1) % 64
            return colctr[0]

        def ld(dst, src, key, q="sp"):
            P.dma(I("dma_start", out=dst, in_=src), w=[key], q=q)
        ld(iota1[:], iota_d, "iota1"); ld(sgc[:], sgc_d, "sgc"); ld(bgc[:], bgc_d, "bgc"); ld(dskc[:], dskc_d, "dskc")
        ld(LSEL[:], lsel_d, "LSEL"); ld(kex[:], kex_d, "kex"); ld(cv[:], cv_d, "cv"); ld(rvi[:], rvi_d, "rvi"); ld(rvc[:], rvc_d, "rvc"); ld(ag[:], ag_d, "ag")
        P.pool(I("memset", identf[:], 1.0), w=["identf"])
        P.pool(I("affine_select", identf[:], identf[:], [[-1, 128]], ALU.is_equal, 0.0, base=0, channel_multiplier=1), r=["identf"], w=["identf"])
        P.pool(I("tensor_copy", ident[:], identf[:]), r=["identf"], w=["ident"])
        P.pool(I("memset", ones_c[:], 1.0), w=["ones_c"])
        P.pool(I("memset", epsc[:], EPS), w=["epsc"])
        P.pool(I("memset", cM0[:], MAGIC), w=["cM"])
        P.pool(I("memset", cM1[:], MAGIC + 0.25), w=["cM"])
        P.pool(I("memset", cMn[:], -MAGIC), w=["cM"])
        P.pool(I("memset", cHP[:], TWO_PI / 4), w=["cM"])

        lamr = SB("lamr_s", [128, 80]); lami = SB("lami_s", [128, 80]); tmpa = SB("tmpa", [128, 80]); tmpb = SB("tmpb", [128, 80])
        tmpc = SB("tmpc", [128, 80]); tmpd = SB("tmpd", [128, 80]); CR = SB("CR", [128, 80]); CI = SB("CI", [128, 80])
        ld(lamr[:], lamr_d, "lamr"); ld(lami[:], lami_d, "lami"); ld(tmpa[:], logdt_d, "tmpa")
        cast_ct = [0]

        def precast(src, dst, ncols, tagname):
            for kc in range(8 if src is not w_glu else 4):
                for c0 in range(0, ncols, 1024):
                    cw = min(1024, ncols - c0)
                    i = cast_ct[0] % 2; cast_ct[0] += 1
                    P.dma(I("dma_start", out=stg[i][:, 0:cw], in_=src[kc * 128:(kc + 1) * 128, c0:c0 + cw]), w=[("stg", i)])
                    if dst is None:
                        P.act(I("activation", WGB[:, kc, 0:cw], stg[i][:, 0:cw], AF.Copy), r=[("stg", i)], w=["WGB"])
                    else:
                        P.act(I("activation", hn[i][:, 0:cw], stg[i][:, 0:cw], AF.Copy), r=[("stg", i)], w=[("hn", i)])
                        P.dma(I("dma_start", out=dst.ap()[kc * 128:(kc + 1) * 128, c0:c0 + cw], in_=hn[i][:, 0:cw]), r=[("hn", i)], w=[tagname], q="pool")
        precast(w_in, winb, 3072, "winb")
        precast(w_out, woutb, 1024, "woutb")
        precast(w_glu, None, 512, "wglu")

        V_ = lambda fn, r, w: P.dve(fn, r, w)
        P.act(I("activation", tmpa[:], tmpa[:], AF.Exp), r=["tmpa"], w=["tmpa"])
        V_(I("tensor_tensor", LR[:], lamr[:], tmpa[:], ALU.mult), ["lamr", "tmpa"], ["LR"])
        V_(I("tensor_tensor", TH[:], lami[:], tmpa[:], ALU.mult), ["lami", "tmpa"], ["TH"])
        P.act(I("activation", RC[:], LR[:], AF.Exp), r=["LR"], w=["RC"])

        def reduce_angle(dst, src_ap, shift, rk, wk, eng="dve", tmp=None, tk=None):
            add = P.dve if eng == "dve" else P.pool
            add(I("tensor_scalar", dst, src_ap, shift, None, ALU.add), r=rk, w=[wk])
            add(I("tensor_scalar", tmp, dst, 1.0 / TWO_PI, MAGIC, ALU.mult, ALU.add), r=[wk], w=[tk])
            add(I("tensor_scalar", tmp, tmp, -MAGIC, -TWO_PI, ALU.add, ALU.mult), r=[tk], w=[tk])
            add(I("tensor_tensor", dst, tmp, dst, ALU.add), r=[tk, wk], w=[wk])
            add(I("tensor_scalar", dst, dst, -3.14159, 3.14159, ALU.max, ALU.min), r=[wk], w=[wk])

        def power(n, outr, outi, tag):
            P.dve(I("tensor_scalar", tmpb[:], TH[:], float(n), None, ALU.mult), r=["TH"], w=["tmpb"])
            reduce_angle(tmpc[:], tmpb[:], 0.0, ["tmpb"], "tmpc", tmp=tmpd[:], tk="tmpd")
            P.act(I("activation", outi, tmpc[:], AF.Sin, scale=0.999996), r=["tmpc"], w=[tag + "i"])
            reduce_angle(tmpc[:], tmpb[:], TWO_PI / 4, ["tmpb"], "tmpc", tmp=tmpd[:], tk="tmpd")
            P.act(I("activation", outr, tmpc[:], AF.Sin, scale=0.999996), r=["tmpc"], w=[tag + "r"])
            P.act(I("activation", tmpc[:], LR[:], AF.Exp, scale=float(n)), r=["LR", tag + "r", tag + "i"], w=["tmpc"])
            P.dve(I("tensor_tensor", outr, outr, tmpc[:], ALU.mult), r=[tag + "r", "tmpc"], w=[tag + "r"])
            P.dve(I("tensor_tensor", outi, outi, tmpc[:], ALU.mult), r=[tag + "i", "tmpc"], w=[tag + "i"])

        A1R = SB("A1R", [128, 80]); A1I = SB("A1I", [128, 80])
        power(1, A1R[:], A1I[:], "A1")
        V_(I("tensor_scalar", tmpa[:], A1R[:], -1.0, None, ALU.add), ["A1r"], ["tmpa"])
        V_(I("tensor_tensor", tmpb[:], lamr[:], lamr[:], ALU.mult), ["lamr"], ["tmpb"])
        V_(I("tensor_tensor", tmpc[:], lami[:], lami[:], ALU.mult), ["lami"], ["tmpc"])
        V_(I("tensor_tensor", tmpb[:], tmpb[:], tmpc[:], ALU.add), ["tmpb", "tmpc"], ["tmpb"])
        V_(I("reciprocal", tmpb[:], tmpb[:]), ["tmpb"], ["tmpb"])
        V_(I("tensor_tensor", tmpc[:], tmpa[:], lamr[:], ALU.mult), ["tmpa", "lamr", "tmpb"], ["tmpc"])
        V_(I("tensor_tensor", tmpd[:], A1I[:], lami[:], ALU.mult), ["A1i", "lami"], ["tmpd"])
        V_(I("tensor_tensor", tmpc[:], tmpc[:], tmpd[:], ALU.add), ["tmpc", "tmpd"], ["tmpc"])
        V_(I("tensor_tensor", CR[:], tmpc[:], tmpb[:], ALU.mult), ["tmpc", "tmpb"], ["CR"])
        V_(I("tensor_tensor", tmpc[:], A1I[:], lamr[:], ALU.mult), ["A1i", "lamr", "CR"], ["tmpc"])
        V_(I("tensor_tensor", tmpd[:], tmpa[:], lami[:], ALU.mult), ["tmpa", "lami", "tmpc"], ["tmpd"])
        V_(I("tensor_tensor", tmpc[:], tmpc[:], tmpd[:], ALU.subtract), ["tmpc", "tmpd"], ["tmpc"])
        V_(I("tensor_tensor", CI[:], tmpc[:], tmpb[:], ALU.mult), ["tmpc", "tmpb"], ["CI"])
        cb = lambda t: t[:, 0:32].unsqueeze(2).to_broadcast([128, 32, 16])
        T1 = SB("T1", [128, 512]); T2 = SB("T2", [128, 512])
        BBR = scr[:, 0:512]; BBI = scr[:, 512:1024]
        ld(BBR, bre_d, "scr"); ld(BBI, bim_d, "scr")
        b3 = lambda t: t.rearrange("p (m c) -> p m c", c=16)
        T3s = SB("T3s", [128, 512])
        V_(I("tensor_tensor", b3(T1[:]), b3(BBR), cb(CR), ALU.mult), ["scr", "CR"], ["T1"])
        V_(I("tensor_tensor", b3(T2[:]), b3(BBI), cb(CI), ALU.mult), ["scr", "CI"], ["T2"])
        V_(I("tensor_tensor", T1[:], T1[:], T2[:], ALU.subtract), ["T1", "T2"], ["T1"])
        V_(I("tensor_tensor", b3(T2[:]), b3(BBI), cb(CR), ALU.mult), ["scr", "CR", "T1"], ["T2"])
        V_(I("tensor_tensor", b3(T3s[:]), b3(BBR), cb(CI), ALU.mult), ["scr", "CI"], ["T3s"])
        V_(I("tensor_tensor", T2[:], T2[:], T3s[:], ALU.add), ["T2", "T3s"], ["T2"])
        ZB = R_A[:, 0:2048].rearrange("p (g c) -> p g c", c=128)

        def build_bc():
            P.dve(I("memset", R_BC[:, 8192:16384], 0.0), w=["CmT"])
            ld(scr[:, 0:512], cre_d, "scr"); ld(scr[:, 512:1024], cim_d, "scr")
            P.dve(I("tensor_scalar", scr[:, 512:1024], scr[:, 512:1024], -1.0, None, ALU.mult), r=["scr"], w=["scr"])
            srcs = {(0, "B"): T1[:], (1, "B"): T2[:], (0, "C"): scr[:, 0:512], (1, "C"): scr[:, 512:1024]}
            for d in range(2):
                for ri in range(2):
                    m0 = (d * 2 + ri) * 16
                    src = srcs[(ri, "C")].rearrange("p (m c) -> p m c", c=16)
                    for j2 in range(2):
                        for gq in range(4):
                            P.dve(I("tensor_copy",
                                CmT[64 * j2:64 * j2 + 64, m0 + gq:m0 + 16:4, 32 * gq + 16 * j2:32 * gq + 16 * j2 + 16],
                                src[64 * j2:64 * j2 + 64, d * 16 + gq:d * 16 + 16:4, :]), r=["scr"], w=["CmT"])
                    srcb = srcs[(ri, "B")].rearrange("p (m c) -> p m c", c=16)
                    P.dve(I("memset", ZB, 0.0), w=["ZB"])
                    for j2 in range(2):
                        for gq in range(4):
                            P.dve(I("tensor_copy",
                                ZB[64 * j2:64 * j2 + 64, gq:16:4, 32 * gq + 16 * j2:32 * gq + 16 * j2 + 16],
                                srcb[64 * j2:64 * j2 + 64, d * 16 + gq:d * 16 + 16:4, :]), r=["T1", "T2"], w=["ZB"])
                    for g4 in range(4):
                        bk = banks[g4 % 2]
                        for j in range(4):
                            gp = g4 * 4 + j
                            P.pe(I("transpose", bk[:, j * 128:(j + 1) * 128], ZB[:, gp, :], identf[:]), r=["ZB", "identf"], w=[("bank", g4 % 2)])
                        P.act(I("activation", BmT[:, m0 + g4 * 4:m0 + g4 * 4 + 4, :], bk[:].rearrange("p (j c) -> p j c", c=128), AF.Copy), r=[("bank", g4 % 2)], w=["BmT"])

        def build_bmats(src_ri, nsel, dst3d, rkeys, wkey):
            for sel in range(nsel):
                for ri in range(2):
                    m0 = (sel * 2 + ri) * 16
                    srcb = src_ri[ri]
                    P.dve(I("memset", ZB, 0.0), w=["ZB"])
                    for j2 in range(2):
                        for gq in range(4):
                            P.dve(I("tensor_copy",
                                ZB[64 * j2:64 * j2 + 64, gq:16:4, 32 * gq + 16 * j2:32 * gq + 16 * j2 + 16],
                                srcb[64 * j2:64 * j2 + 64, sel * 16 + gq:sel * 16 + 16:4, :]), r=rkeys, w=["ZB"])
                    for g4 in range(4):
                        bk = banks[g4 % 2]
                        for j in range(4):
                            gp = g4 * 4 + j
                            P.pe(I("transpose", bk[:, j * 128:(j + 1) * 128], ZB[:, gp, :], identf[:]), r=["ZB", "identf"], w=[("bank", g4 % 2)])
                        P.act(I("activation", dst3d[:, m0 + g4 * 4:m0 + g4 * 4 + 4, :], bk[:].rearrange("p (j c) -> p j c", c=128), AF.Copy), r=[("bank", g4 % 2)], w=[wkey])

        BmTc = R_BC[:, 0:12288].rearrange("p (m c) -> p m c", c=128)

        def build_bc_carry():
            f = lambda o: R_A[:, 2048 + 768 * o:2048 + 768 * (o + 1)]
            BRc, BIc, U1, U2, U3 = f(0), f(1), f(2), f(3), f(4)
            ld(BRc, brec_d, "BRc"); ld(BIc, bimc_d, "BIc")
            c3 = lambda t: t.rearrange("p (m c) -> p m c", c=16)
            cbc = lambda t: t[:, 32:80].unsqueeze(2).to_broadcast([128, 48, 16])
            P.dve(I("tensor_tensor", c3(U1), c3(BRc), cbc(CR), ALU.mult), r=["BRc", "CR"], w=["U1"])
            P.dve(I("tensor_tensor", c3(U2), c3(BIc), cbc(CI), ALU.mult), r=["BIc", "CI"], w=["U2"])
            P.dve(I("tensor_tensor", U1, U1, U2, ALU.subtract), r=["U1", "U2"], w=["U1"])
            P.dve(I("tensor_tensor", c3(U2), c3(BIc), cbc(CR), ALU.mult), r=["BIc", "CR", "U1"], w=["U2"])
            P.dve(I("tensor_tensor", c3(U3), c3(BRc), cbc(CI), ALU.mult), r=["BRc", "CI"], w=["U3"])
            P.dve(I("tensor_tensor", U2, U2, U3, ALU.add), r=["U2", "U3"], w=["U2"])
            build_bmats([c3(U1), c3(U2)], 3, BmTc, ["U1", "U2"], "BmTc")

        def rstd_from_ss(ss_ap, n, key_r, key_w):
            P.act(I("activation", ss_ap, ss_ap, AF.Sqrt, bias=epsc[:, 0:1], scale=1.0 / n), r=[key_r, "epsc"], w=[key_r])
            P.dve(I("reciprocal", ss_ap, ss_ap), r=[key_r], w=[key_r])

        tilectr = [0]

        def norm_tiles(ci, tok0, ntiles, src=None, t0=0, bankbase=6):
            src = xc[ci] if src is None else src
            for t in range(ntiles):
                i = tilectr[0] % 2; tilectr[0] += 1
                c = newcol()
                P.dma(I("dma_start", out=stg[i][:], in_=src[tok0 + t * 128:tok0 + (t + 1) * 128, :]), w=[("stg", i)])
                P.act(I("activation", hn[i][:], stg[i][:], AF.Square), r=[("stg", i)], w=[("hn", i)])
                P.dve(I("reduce_sum", colt[:, c:c + 1], hn[i][:], axis=AX.X), r=[("hn", i)], w=[("col", c)])
                rstd_from_ss(colt[:, c:c + 1], 1024.0, ("col", c), ("col", c))
                P.dve(I("scalar_tensor_tensor", hn[i][:], stg[i][:], colt[:, c:c + 1], gvec[:], ALU.mult, ALU.mult),
                      r=[("stg", i), ("col", c), "gvec"], w=[("hn", i)])
                bn = bankbase + ((t + t0) % 2)
                bk = banks[bn]
                bkb = bk[:].bitcast(BF16)
                for kc in range(8):
                    P.pe(I("transpose", bkb[:, kc * 128:(kc + 1) * 128], hn[i][:, kc * 128:(kc + 1) * 128], ident[:]),
                         r=[("hn", i), "ident"], w=[("bank", bn)])
                P.act(I("activation", hT[:, :, (t + t0) * 128:(t + t0 + 1) * 128], bkb.rearrange("p (k c) -> p k c", c=128), AF.Copy),
                      r=[("bank", bn)], w=[("hT", t + t0)])

        wsctr = [0]

        def load_wgroup(cg):
            i = wsctr[0] % 2; wsctr[0] += 1
            P.dma(I("dma_start", out=WS[i][:], in_=winb.ap()[:, cg * 512:(cg + 1) * 512].rearrange("(k p) c -> p k c", p=128)),
                  r=["winb"], w=[("WS", i)])
            return i

        pbank = [0]

        def fm_proj(cg, tok_list, evac):
            wi = load_wgroup(cg)
            for cc in range(4):
                for (h0, n, dst0) in tok_list:
                    b = pbank[0] % 4; pbank[0] += 1
                    bk = banks[b]
                    tl = list(range(h0 // 128, (h0 + n + 127) // 128))
                    for kc in range(8):
                        P.pe(I("matmul", bk[:, 0:n], WS[wi][:, kc, cc * 128:(cc + 1) * 128], hT[:, kc, h0:h0 + n], start=(kc == 0), stop=(kc == 7)),
                             r=[("WS", wi)] + [("hT", t) for t in tl], w=[("bank", b)])
                    evac(cc, bk, b, n, dst0)

        def tm_proj(cg, tiles, evac):
            wi = load_wgroup(cg)
            for t in tiles:
                b = pbank[0] % 4; pbank[0] += 1
                bk = banks[b]
                for kc in range(8):
                    P.pe(I("matmul", bk[:], hT[:, kc, t * 128:(t + 1) * 128], WS[wi][:, kc, :], start=(kc == 0), stop=(kc == 7)),
                         r=[("WS", wi), ("hT", t)], w=[("bank", b)])
                evac(t, bk, b)

        sigt = [SB("sigt%d" % i, [128, 512], BF16) for i in range(2)]
        sgctr = [0]

        def build_masks():
            RP = R_A[:, 0:7680].rearrange("p (h r c) -> p h r c", h=8, r=15)
            for a in range(2):
                for h in range(8):
                    src = bass.AP(rpbp_d.tensor, (h * 16 + a) * 127, [[1, 64], [127, 15], [1, 64]])
                    P.dma(I("dma_start", out=RP[64 * a:64 * a + 64, h, :, :], in_=src), w=[("RP", a, h)])
            P.act(I("activation", R_B[:, 20576:28256], R_A[:, 0:7680], AF.Exp), r=[("RP", a, h) for a in range(2) for h in range(8)], w=["RPE"])
            for b in range(2):
                r0 = 1 - b
                P.dve(I("tensor_tensor", MB[:, :, :, b * 64:(b + 1) * 64], RPE[:, :, r0:r0 + 13:2, ::-1],
                                                             cv[:].unsqueeze(1).unsqueeze(1).to_broadcast([128, 8, 7, 64]), ALU.mult),
                      r=["RPE", "cv"], w=["MB"])
            P.dve(I("tensor_tensor", MBI.rearrange("p h o (b c) -> p h (o b) c", b=2), MB[:, :, 1:6, :].rearrange("p h o (b c) -> p h (o b) c", b=2),
                                            rvi[:].unsqueeze(1).unsqueeze(3).to_broadcast([128, 8, 10, 64]), ALU.mult),
                  r=["MB", "rvi"], w=["MBI"])

        PT = [R_B[:, 28256 + 768 * i:28256 + 768 * (i + 1)] for i in range(2)]
        YA = R_B[:, 29792:30816].bitcast(F32)
        MA = R_B[:, 30816:31328]
        YO = R_B[:, 31328:32368].bitcast(F32)
        RD = SB("RD", [128, 8], F32)

        def attention_half(ci, hh):
            base_tok = 1024 * hh
            P.dma(I("dma_start", out=gvec[:], in_=ng_d), w=["gvec"])
            norm_tiles(ci, base_tok, 12)
            def ev_q(cc, bk, b, n, dst0):
                P.act(I("activation", qT[:, cc, dst0:dst0 + n], bk[:, 0:n], AF.Copy, scale=0.125), r=[("bank", b)], w=[("qT", cc, dst0)])
            fm_proj(2, [(256, 512, 0), (768, 512, 512)], ev_q)
            def ev_k(cc, bk, b, n, dst0):
                P.dve(I("tensor_copy", kT[:, cc, dst0:dst0 + n], bk[:, 0:n]), r=[("bank", b)], w=[("kT", cc, dst0)])
            fm_proj(3, [(0, 512, 0), (512, 512, 512), (1024, 512, 1024)], ev_k)
            def ev_v(t, bk, b):
                P.act(I("activation", VA[:, t, :, 0:64], bk[:].rearrange("p (h c) -> p h c", c=64), AF.Copy), r=[("bank", b)], w=[("VA", t)])
                kt = ci * 20 + 8 * hh + t
                P.dve(I("tensor_copy", VA[:, t, :, 64:65], kex[:, kt:kt + 1].unsqueeze(1).to_broadcast([128, 8, 1])), r=["kex", ("VA", t)], w=[("VA", t)])
            tm_proj(4, list(range(12)), ev_v)
            def ev_z(t, bk, b):
                i = sgctr[0] % 2; sgctr[0] += 1
                P.act(I("activation", sigt[i][:], bk[:], AF.Sigmoid), r=[("bank", b)], w=[("sigt", i)])
                P.dve(I("tensor_tensor", scr[:, 0:512], bk[:], sigt[i][:], ALU.mult), r=[("bank", b), ("sigt", i)], w=["scr"])
                P.dve(I("tensor_tensor", ZAG[:, t - 2, :], scr[:, 0:512], ag[:], ALU.mult), r=["scr", "ag"], w=[("ZAG", t - 2)])
            tm_proj(5, list(range(2, 10)), ev_z)
            def geom(jpl):
                jp = 8 * hh + jpl
                olist = [-4, -2, 0, 2, 4]
                if jp == 0:
                    olist = olist + [6]
                if jp == 15:
                    olist = [-6] + olist
                return jp, olist, len(olist), jp in (0, 1, 14, 15), {0: 0, 1: 1, 14: 2, 15: 3}.get(jp, 0), (olist[0] + 6) // 2
            units = [(jpl, h) for jpl in range(8) for h in range(8)]

            def emit_S(ui):
                jpl, h = units[ui]
                jp, olist, n, boundary, jpb, os0 = geom(jpl)
                hp, hb = h // 2, 64 * (h % 2)
                bi = ui % 2
                sb0, sb1 = banks[2 * bi], banks[2 * bi + 1]
                for i, o in enumerate(olist):
                    kt = jpl + o // 2 + 2
                    dst = (sb0 if i < 4 else sb1)[:, (i % 4) * 128:(i % 4) * 128 + 128]
                    P.pe(I("matmul", dst, kT[hb:hb + 64, hp, kt * 128:(kt + 1) * 128], qT[hb:hb + 64, hp, jpl * 128:(jpl + 1) * 128], start=True, stop=True),
                         r=[("kT", hp, 512 * (kt // 4)), ("qT", hp, 512 * (jpl // 4))], w=[("bank", 2 * bi + (0 if i < 4 else 1))])

            def emit_softmax(ui):
                jpl, h = units[ui]
                jp, olist, n, boundary, jpb, os0 = geom(jpl)
                bi = ui % 2
                sb0, sb1 = banks[2 * bi], banks[2 * bi + 1]
                pt = PT[bi]
                P.act(I("activation", pt[:, 0:512], sb0[:], AF.Exp), r=[("bank", 2 * bi)], w=[("PT", bi)])
                P.act(I("activation", pt[:, 512:128 * n], sb1[:, 0:128 * (n - 4)], AF.Exp), r=[("bank", 2 * bi + 1)], w=[("PT", bi)])
                if not boundary:
                    P.dve(I("tensor_tensor", pt[:, 0:640], pt[:, 0:640], MBI[:, h, :, :].rearrange("p o c -> p (o c)"), ALU.mult),
                          r=[("PT", bi), "MBI"], w=[("PT", bi)])
                else:
                    P.dve(I("tensor_tensor", pt[:, 0:128 * n], pt[:, 0:128 * n], MB[:, h, os0:os0 + n, :].rearrange("p o c -> p (o c)"), ALU.mult),
                          r=[("PT", bi), "MB"], w=[("PT", bi)])
                    r0 = (ci * 4 + jpb) * 14 + 2 * os0
                    P.dve(I("tensor_tensor", pt[:, 0:128 * n].rearrange("p (x c) -> p x c", c=64), pt[:, 0:128 * n].rearrange("p (x c) -> p x c", c=64),
                            rvc[:, r0:r0 + 2 * n].unsqueeze(2).to_broadcast([128, 2 * n, 64]), ALU.mult),
                          r=[("PT", bi), "rvc"], w=[("PT", bi)])

            def emit_PV(ui):
                jpl, h = units[ui]
                jp, olist, n, boundary, jpb, os0 = geom(jpl)
                bi = ui % 2
                pt = PT[bi]
                ob = banks[4 + h // 4]
                for i, o in enumerate(olist):
                    kt = jpl + o // 2 + 2
                    P.pe(I("matmul", ob[:, (h % 4) * 65:(h % 4) * 65 + 65], pt[:, i * 128:(i + 1) * 128], VA[:, kt, h, :], start=(i == 0), stop=(i == n - 1)),
                         r=[("PT", bi), ("VA", kt)], w=[("bank", 4 + h // 4)])
                if h % 4 == 3:
                    hb2 = h // 4
                    P.act(I("activation", YO[:, 260 * hb2:260 * hb2 + 260], ob[:, 0:260], AF.Copy), r=[("bank", 4 + hb2)], w=[("YO", hb2)])

            def emit_post(jpl):
                jp = 8 * hh + jpl
                for hb2 in range(2):
                    o3 = YO[:, 260 * hb2:260 * hb2 + 260].rearrange("p (h c) -> p h c", c=65)
                    P.dve(I("reciprocal", RD[:, 4 * hb2:4 * hb2 + 4], o3[:, :, 64]), r=[("YO", hb2)], w=[("RD", hb2)])
                    P.dve(I("tensor_tensor", YA[:, 256 * hb2:256 * hb2 + 256].rearrange("p (h c) -> p h c", c=64), o3[:, :, 0:64],
                            RD[:, 4 * hb2:4 * hb2 + 4].unsqueeze(2).to_broadcast([128, 4, 64]), ALU.mult),
                          r=[("YO", hb2), ("RD", hb2)], w=[("YA", hb2)])
                c = newcol()
                P.act(I("activation", scr[:, 0:512], YA[:], AF.Square), r=[("YA", 0), ("YA", 1)], w=["scr"])
                P.dve(I("reduce_sum", colt[:, c:c + 1], scr[:, 0:512], axis=AX.X), r=["scr"], w=[("col", c)])
                rstd_from_ss(colt[:, c:c + 1], 512.0, ("col", c), ("col", c))
                P.dve(I("scalar_tensor_tensor", MA[:], YA[:], colt[:, c:c + 1], ZAG[:, jpl, :], ALU.mult, ALU.mult),
                      r=[("YA", 0), ("YA", 1), ("col", c), ("ZAG", jpl)], w=["MA"])
                bk = banks[6 + (jpl % 2)]
                bkb = bk[:].bitcast(BF16)
                for kc in range(4):
                    P.pe(I("transpose", bkb[:, kc * 128:(kc + 1) * 128], MA[:, kc * 128:(kc + 1) * 128], ident[:]), r=["MA", "ident"], w=[("bank", 6 + (jpl % 2))])
                P.act(I("activation", MAT[:, :, jp * 128:(jp + 1) * 128], bkb[:, 0:512].rearrange("p (k c) -> p k c", c=128), AF.Copy),
                      r=[("bank", 6 + (jpl % 2))], w=[("MAT", jp)])

            emit_S(0)
            for ui in range(len(units)):
                if ui + 1 < len(units):
                    emit_S(ui + 1)
                emit_softmax(ui)
                emit_PV(ui)
                if units[ui][1] == 7:
                    emit_post(units[ui][0])

        _o = [16384]
        def carve(n, dt=BF16):
            k = n if dt == BF16 else 2 * n
            v = R_B[:, _o[0]:_o[0] + k]
            _o[0] += k
            return v if dt == BF16 else v.bitcast(F32)
        CT = [carve(512) for i in range(4)]
        SN = [carve(1024) for i in range(4)]
        BU = [carve(1024) for i in range(2)]
        GI2 = [carve(1024) for i in range(2)]; GG2 = [carve(1024) for i in range(2)]
        TA2 = [carve(1024) for i in range(2)]
        HH = [carve(1024) for i in range(2)]
        assert _o[0] <= 32768, _o[0]
        ANG = scr[:, 0:512]; ANG3 = scr[:, 512:1024]
        v3 = lambda ap: ap.rearrange("p (a c) -> p a c", a=2)
        swp = lambda ap: v3(ap)[:, ::-1, :]
        tabctr = [0]

        def tables_gen(cols, par):
            for c in range(2):
                idx = cols[c]
                ti = par * 2 + c
                P.act(I("activation", ANG[:], iota1[:], AF.Copy, scale=TH[:, idx:idx + 1]), r=["iota1", "TH"], w=["scr"])
                yield
                for which in (1, 0):
                    if which == 0:
                        P.act(I("activation", ANG[:], ANG[:], AF.Identity, bias=cHP[:, 0:1], scale=1.0), r=["scr", "cM"], w=["scr"])
                        yield
                    P.act(I("activation", ANG3[:], ANG[:], AF.Identity, bias=cM0[:, 0:1], scale=1.0 / TWO_PI), r=["scr", "cM"], w=["scr"])
                    yield
                    P.act(I("activation", ANG3[:], ANG3[:], AF.Identity, bias=cMn[:, 0:1], scale=1.0), r=["scr", "cM"], w=["scr"])
                    yield
                    P.dve(I("scalar_tensor_tensor", ANG3[:], ANG3[:], -TWO_PI, ANG[:], ALU.mult, ALU.add), r=["scr"], w=["scr"])
                    yield
                    P.dve(I("tensor_scalar", ANG3[:], ANG3[:], -3.14159, 3.14159, ALU.max, ALU.min), r=["scr"], w=["scr"])
                    yield
                    if which == 1:
                        P.act(I("activation", SN[ti][:, 0:512], ANG3[:], AF.Sin, scale=0.999996), r=["scr"], w=[("SN", ti)])
                        yield
                        P.act(I("activation", SN[ti][:, 512:1024], ANG3[:], AF.Sin, scale=-0.999996), r=["scr"], w=[("SN", ti)])
                        yield
                    else:
                        P.act(I("activation", CT[ti][:], ANG3[:], AF.Sin, scale=0.999996), r=["scr"], w=[("CT", ti)])
                        yield

        def run_groups(groups):
            for n, g in enumerate(groups):
                if g.get("pre") is not None:
                    g["pre"]()
                if n == 0:
                    for _ in tables_gen(g["cols"], 0):
                        pass
                gens = g["chains"](n % 2)
                if g.get("side") is not None:
                    gens.append(g["side"]())
                if n + 1 < len(groups):
                    gens.append(tables_gen(groups[n + 1]["cols"], (n + 1) % 2))
                run_chains(gens)
                if g.get("post") is not None:
                    g["post"]()

        def scan_chain(c, rev, col, bm, bmkey, ub, ukey, kcp, par, mode, init0=None, cm=None, ystate=None, gcol=None):
            ti = par * 2 + c
            CC = CT[ti][:].unsqueeze(1).to_broadcast([128, 2, 512])
            tk = [("CT", ti), ("SN", ti)]
            TA, GI, GG, H2, BUd = TA2[c], GI2[c], GG2[c], HH[c], BU[c]
            rb = RC[:, col:col + 1].to_broadcast([128, 512])
            for kk in range(4):
                k = kk if not rev else 3 - kk
                for ri in range(2):
                    bk = banks[4 + 2 * c + ri]
                    P.pe(I("matmul", bk[:], bm(ri), ub[:, kcp, 512 * k:512 * k + 512], start=True, stop=True),
                         r=[bmkey, (ukey, kcp, k)], w=[("bank", 4 + 2 * c + ri)])
                    src = bk[:] if not rev else bk[:, ::-1]
                    P.act(I("activation", BUd[:, ri * 512:(ri + 1) * 512], src, AF.Copy), r=[("bank", 4 + 2 * c + ri)], w=[("BU", c)])
                yield
                P.dve(I("tensor_tensor", v3(TA), v3(BUd), CC, ALU.mult), r=[("BU", c)] + tk, w=[("TA", c)])
                yield
                P.dve(I("tensor_tensor", v3(GI), swp(BUd), v3(SN[ti]), ALU.mult), r=[("BU", c)] + tk, w=[("GI", c)])
                yield
                P.dve(I("tensor_tensor", GI[:], TA[:], GI[:], ALU.add), r=[("TA", c), ("GI", c)], w=[("GI", c)])
                yield
                for ri in range(2):
                    if mode == "own":
                        if kk == 0:
                            init, ir = init0(ri)
                        else:
                            c0 = ri * 512 + (511 if not rev else 0)
                            init, ir = H2[:, c0:c0 + 1], [("HH", c)]
                    else:
                        init, ir = HLc[:, ri, gcol:gcol + 1], [("HLc", gcol)]
                    P.dve(I("tensor_tensor_scan", GG[:, ri * 512:(ri + 1) * 512], rb, GI[:, ri * 512:(ri + 1) * 512], init, ALU.mult, ALU.add),
                          r=[("GI", c), "RC"] + ir, w=[("GG", c)])
                    yield
                if mode == "own":
                    hout = v3(H2) if not rev else v3(H2)[:, :, ::-1]
                    P.dve(I("tensor_tensor", v3(TA), v3(GG), CC, ALU.mult), r=[("GG", c)] + tk, w=[("TA", c)])
                    yield
                    P.dve(I("tensor_tensor", v3(GI), swp(GG), swp(SN[ti]), ALU.mult), r=[("GG", c)] + tk, w=[("GI", c)])
                    yield
                    P.dve(I("tensor_tensor", hout, v3(TA), v3(GI), ALU.add), r=[("TA", c), ("GI", c)], w=[("HH", c)])
                    yield
                    for ri in range(2):
                        ystate[k] += 1
                        P.pe(I("matmul", banks[k][:], cm(ri), H2[:, ri * 512:(ri + 1) * 512], start=(ystate[k] == 1), stop=(ystate[k] == 16)),
                             r=["CmT", ("HH", c)], w=[("bank", k)])
                    yield
                else:
                    t1, t2 = (TL1, TL2) if c == 0 else (TL3, TL4)
                    P.dve(I("tensor_tensor", t2[:].unsqueeze(2), swp(GG)[:, :, 511:512], swp(SN[ti])[:, :, 511:512], ALU.mult), r=[("GG", c)] + tk, w=[("TL", c, 1)])
                    yield
                    P.dve(I("scalar_tensor_tensor", HLc[:, :, gcol:gcol + 1], v3(GG)[:, :, 511:512], CT[ti][:, 511:512], t2[:].unsqueeze(2), ALU.mult, ALU.add),
                          r=[("GG", c), ("TL", c, 1)] + tk, w=[("HLc", gcol)])
                    yield

        def run_chains(gens):
            gens = list(gens)
            while gens:
                for g in list(gens):
                    try:
                        next(g)
                    except StopIteration:
                        gens.remove(g)

        def ssm_chunk(ci, correction):
            groups = []
            for kcp in range(4):
                ystate = {k: 0 for k in range(4)}
                for gq in range(4):
                    gp = 4 * kcp + gq
                    def chains(par, gp=gp, kcp=kcp, ystate=ystate):
                        out = []
                        for d in range(2):
                            idx = d * 16 + gp
                            if ci == 1:
                                init0 = (lambda ri, idx=idx: (HFB[:, ri, idx:idx + 1], ["HFB"]))
                            else:
                                init0 = (lambda ri: (0.0, []))
                            out.append(scan_chain(d, d == 1, idx,
                                                  (lambda ri, d=d, gp=gp: BmT[:, (d * 2 + ri) * 16 + gp, :]), "BmT", uT, "uT", kcp, par, "own",
                                                  init0=init0, cm=(lambda ri, d=d, gp=gp: CmT[:, (d * 2 + ri) * 16 + gp, :]), ystate=ystate))
                        return out
                    post = None
                    if gq == 3:
                        def post(kcp=kcp):
                            for k in range(4):
                                P.dve(I("scalar_tensor_tensor", Y0T[:, kcp, 512 * k:512 * k + 512], uT[:, kcp, 512 * k:512 * k + 512], dskc[:, kcp:kcp + 1], banks[k][:], ALU.mult, ALU.add),
                                      r=[("bank", k), ("uT", kcp, k), "dskc"], w=[("Y0T", kcp, k)] + [("hT", t_) for t_ in range(16)])
                    groups.append(dict(cols=(gp, 16 + gp), chains=chains, post=post))
            run_groups(groups)

        def carry_scan():
            toks = [(512 * k, 512, 512 * k) for k in range(4)]
            hl_all = [("HLc", g) for g in range(16)]
            P.dve(I("memset", HLc[:], 0.0), w=hl_all)
            P.dve(I("memset", HFB[:], 0.0), w=["HFB"])
            groups = []
            UB = [(uT, "uT"), (ZSG, "ZSG")]
            def proj_gen(sl):
                ub, ukey = UB[sl % 2]
                P.dma(I("dma_start", out=gvec[:], in_=ng_d), w=["gvec"])
                for t in range(16):
                    norm_tiles(1, 2048 * sl + 128 * t, 1, src=xoth, t0=t, bankbase=0)
                    yield
                wi = load_wgroup(0)
                cnt = 0
                for cc in range(4):
                    for (h0, n, dst0) in toks:
                        b = 2 + cnt % 2; cnt += 1
                        bk = banks[b]
                        for kc in range(8):
                            P.pe(I("matmul", bk[:, 0:n], WS[wi][:, kc, cc * 128:(cc + 1) * 128], hT[:, kc, h0:h0 + n], start=(kc == 0), stop=(kc == 7)),
                                 r=[("WS", wi)] + [("hT", t) for t in range(h0 // 128, (h0 + n) // 128)], w=[("bank", b)])
                        P.act(I("activation", ub[:, cc, dst0:dst0 + n], bk[:, 0:n], AF.Copy), r=[("bank", b)], w=[(ukey, cc, dst0 // 512)])
                        yield
            for _ in proj_gen(0):
                pass
            for sl in range(3):
                def post(sl=sl):
                    for dsel in range(2):
                        P.dve(I("scalar_tensor_tensor", HFB[:, :, dsel * 16:dsel * 16 + 16], HLc[:], LSEL[:, 3 + dsel * 3 + sl:4 + dsel * 3 + sl],
                                HFB[:, :, dsel * 16:dsel * 16 + 16], ALU.mult, ALU.add), r=hl_all + ["LSEL", "HFB"], w=["HFB"])
                    if sl < 2:
                        P.dve(I("tensor_scalar", HLc[:], HLc[:], LSEL[:, sl + 1:sl + 2], None, ALU.mult), r=hl_all + ["LSEL"], w=hl_all)
                for g2 in range(8):
                    gps = (2 * g2, 2 * g2 + 1)
                    def chains(par, gps=gps, sl=sl):
                        out = []
                        for c in range(2):
                            gp = gps[c]
                            out.append(scan_chain(c, False, 32 + sl * 16 + gp,
                                                  (lambda ri, gp=gp, sl=sl: BmTc[:, (sl * 2 + ri) * 16 + gp, :]), "BmTc", UB[sl % 2][0], UB[sl % 2][1], gp // 4, par, "carry", gcol=gp))
                        return out
                    groups.append(dict(cols=(32 + sl * 16 + gps[0], 32 + sl * 16 + gps[1]), chains=chains,
                                       side=((lambda sl=sl: proj_gen(sl + 1)) if (g2 == 0 and sl < 2) else None), post=(post if g2 == 7 else None)))
            run_groups(groups)

        def ssm_project(ci):
            P.dma(I("dma_start", out=gvec[:], in_=ng_d), w=["gvec"])
            norm_tiles(ci, OWN0, 16)
            toks = [(512 * k, 512, 512 * k) for k in range(4)]
            def ev_u(cc, bk, b, n, dst0):
                P.act(I("activation", uT[:, cc, dst0:dst0 + n], bk[:, 0:n], AF.Copy), r=[("bank", b)], w=[("uT", cc, dst0 // 512)])
            fm_proj(0, toks, ev_u)
            def ev_zs(cc, bk, b, n, dst0):
                i = sgctr[0] % 2; sgctr[0] += 1
                P.act(I("activation", sigt[i][:, 0:n], bk[:, 0:n], AF.Sigmoid), r=[("bank", b)], w=[("sigt", i)])
                P.dve(I("scalar_tensor_tensor", ZSG[:, cc, dst0:dst0 + n], bk[:, 0:n], sgc[:, cc:cc + 1], sigt[i][:, 0:n], ALU.mult, ALU.mult),
                      r=[("bank", b), ("sigt", i), "sgc"], w=[("ZSG", cc, dst0 // 512)])
            fm_proj(1, toks, ev_zs)

        XA = scr[:, 0:512]; XB = scr[:, 512:1024]

        def ssm_post(ci):
            for kcp in range(4):
                for k in range(4):
                    y = Y0T[:, kcp, 512 * k:512 * k + 512]
                    yk = ("Y0T", kcp, k)
                    gi_ = (kcp * 4 + k) % 2
                    xa, xb = (XA, XB) if gi_ == 0 else (stg[0][:, 0:512], stg[0][:, 512:1024])
                    ka, kb = ("XA", gi_), ("XB", gi_)
                    P.dve(I("tensor_tensor", xa, y, y, ALU.mult), r=[yk], w=[ka])
                    P.dve(I("tensor_scalar", xa, xa, 0.044715, 1.0, ALU.mult, ALU.add), r=[ka], w=[ka])
                    P.dve(I("tensor_tensor", xa, xa, y, ALU.mult), r=[ka, yk], w=[ka])
                    P.act(I("activation", xb, xa, AF.Sigmoid, scale=1.5957691216057308), r=[ka], w=[kb])
                    P.dve(I("tensor_tensor", YGT[:, kcp, 512 * k:512 * k + 512], y, xb, ALU.mult), r=[kb, yk], w=[("YGT", kcp, k)])
            for k in range(4):
                for kc2 in range(4):
                    b = pbank[0] % 4; pbank[0] += 1
                    bk = banks[b]
                    for kcp in range(4):
                        P.pe(I("matmul", bk[:], WGB[:, kcp, kc2 * 128:(kc2 + 1) * 128], YGT[:, kcp, 512 * k:512 * k + 512], start=(kcp == 0), stop=(kcp == 3)),
                             r=["WGB"] + [("YGT", q, k) for q in range(4)], w=[("bank", b)])
                    i = sgctr[0] % 2; sgctr[0] += 1
                    P.act(I("activation", sigt[i][:], bk[:], AF.Sigmoid, bias=bgc[:, kc2:kc2 + 1]), r=[("bank", b), "bgc"], w=[("sigt", i)])
                    ysl = YGT[:, kc2, 512 * k:512 * k + 512]
                    P.dve(I("tensor_tensor", sigt[i][:], sigt[i][:], ysl, ALU.mult), r=[("sigt", i), ("YGT", kc2, k)], w=[("sigt", i)])
                    P.dve(I("tensor_tensor", Y2Q[:, kc2, 512 * k:512 * k + 512], sigt[i][:], sigt[i][:], ALU.mult), r=[("sigt", i)], w=[("Y2Q", kc2, k)])
                    P.dve(I("tensor_tensor", ZSG[:, kc2, 512 * k:512 * k + 512], ZSG[:, kc2, 512 * k:512 * k + 512], sigt[i][:], ALU.mult),
                          r=[("sigt", i), ("ZSG", kc2, k)], w=[("ZSG", kc2, k)])
            for tt in range(16):
                bk = banks[6 + tt % 2]
                for kc2 in range(4):
                    P.pe(I("matmul", bk[:, 0:1], Y2Q[:, kc2, tt * 128:(tt + 1) * 128], ones_c[:], start=(kc2 == 0), stop=(kc2 == 3)),
                         r=[("Y2Q", kc2, tt // 4), "ones_c"], w=[("bank", 6 + tt % 2)])
                P.act(I("activation", RSTD_S[:, tt:tt + 1], bk[:, 0:1], AF.Sqrt, bias=epsc[:, 0:1], scale=1.0 / 512), r=[("bank", 6 + tt % 2), "epsc"], w=[("RS", tt)])
                P.dve(I("reciprocal", RSTD_S[:, tt:tt + 1], RSTD_S[:, tt:tt + 1]), r=[("RS", tt)], w=[("RS", tt)])

        OT = [R_A[:, 1024 * i:1024 * (i + 1)] for i in range(2)]

        def out_proj(ci):
            P.dma(I("dma_start", out=gvec[:], in_=fg_d), w=["gvec"])
            wis = []
            for half in range(2):
                i = wsctr[0] % 2; wsctr[0] += 1
                P.dma(I("dma_start", out=WS[i][:], in_=woutb.ap()[:, half * 512:(half + 1) * 512].rearrange("(k p) c -> p k c", p=128)), r=["woutb"], w=[("WS", i)])
                wis.append(i)
            for tt in range(16):
                i = tilectr[0] % 2; tilectr[0] += 1
                P.dma(I("dma_start", out=stg[i][:], in_=xc[ci, OWN0 + tt * 128:OWN0 + (tt + 1) * 128, :]), w=[("stg", i)])
                acc = scr if tt % 2 == 0 else R_A[:, 2048:3072]
                ak = tt % 2
                for half in range(2):
                    bs, ba = banks[half], banks[2 + half]
                    for kc in range(4):
                        P.pe(I("matmul", bs[:], ZSG[:, kc, tt * 128:(tt + 1) * 128], WS[wis[half]][:, kc, :], start=(kc == 0), stop=(kc == 3)),
                             r=[("ZSG", kc, tt // 4), ("WS", wis[half])], w=[("bank", half)])
                    for kc in range(4):
                        P.pe(I("matmul", ba[:], MAT[:, kc, tt * 128:(tt + 1) * 128], WS[wis[half]][:, 4 + kc, :], start=(kc == 0), stop=(kc == 3)),
                             r=[("MAT", tt), ("WS", wis[half])], w=[("bank", 2 + half)])
                    hs = slice(512 * half, 512 * half + 512)
                    P.dve(I("scalar_tensor_tensor", acc[:, hs], bs[:], RSTD_S[:, tt:tt + 1], stg[i][:, hs], ALU.mult, ALU.add),
                          r=[("bank", half), ("RS", tt), ("stg", i)], w=[("scrh", ak, half)])
                    P.dve(I("tensor_tensor", acc[:, hs], acc[:, hs], ba[:], ALU.add), r=[("bank", 2 + half), ("scrh", ak, half)], w=[("scrh", ak, half)])
                c = newcol()
                oi = tt % 2
                P.act(I("activation", OT[oi][:], acc[:], AF.Square), r=[("scrh", ak, 0), ("scrh", ak, 1)], w=[("OT", oi)])
                P.dve(I("reduce_sum", colt[:, c:c + 1], OT[oi][:], axis=AX.X), r=[("OT", oi)], w=[("col", c)])
                rstd_from_ss(colt[:, c:c + 1], 1024.0, ("col", c), ("col", c))
                P.dve(I("scalar_tensor_tensor", OT[oi][:], acc[:], colt[:, c:c + 1], gvec[:], ALU.mult, ALU.mult),
                      r=[("scrh", ak, 0), ("scrh", ak, 1), ("col", c), "gvec"], w=[("OT", oi)])
                P.dma(I("dma_start", out=yo[ci, tt * 128:(tt + 1) * 128, :], in_=OT[oi][:]), r=[("OT", oi)], w=[("yo", ci, tt)])

        SR = scr[:, 0:512].rearrange("p (r c) -> p r c", c=64); TW = scr[:, 512:1024].rearrange("p (r c) -> p r c", c=64); TK = SB("TK", [128, 3, 64])
        HIN = SB("HIN", [128, 64])

        def carry_exchange():
            ld(WSEL.rearrange("p k r c -> p (k r c)"), wsel_d.rearrange("p (k r c) -> p k r c", k=3, r=8).rearrange("p k r c -> p (k r c)"), "WSEL")
            P.dma(I("dma_start", out=ccin.ap(), in_=SPK[:]), r=[("SPK", c) for c in range(64)], w=["ccin"], q="pool")
            def cc(e):
                ins = e.collective_compute("AllGather", ALU.bypass, replica_groups=[list(range(8))], ins=[ccin.ap().opt()], outs=[ccout.ap().opt()])
                ins.then_inc(cc_sem)
                e.wait_ge(cc_sem, 1)
                return e.nop()
            P.pool(cc, r=["ccin"], w=["ccout"])
            P.dma(I("dma_start", out=SR[:], in_=ccout.ap().rearrange("(r p) c -> p r c", p=128)), r=["ccout"], w=["SR"], q="pool")
            for k in range(3):
                P.dve(I("tensor_tensor", TW[:], SR[:], WSEL[:, k, :, :], ALU.mult), r=["SR", "WSEL"], w=["TW"])
                P.dve(I("reduce_sum", TK[:, k, :], TW[:].rearrange("p r c -> p c r"), axis=AX.X), r=["TW"], w=[("TK", k)])
            t_re = lambda k: TK[:, k, 0:32]
            t_im = lambda k: TK[:, k, 32:64]
            P.dve(I("tensor_copy", HIN[:], TK[:, 0, :]), r=[("TK", 0)], w=["HIN"])
            for k, pi_ in ((1, 3), (2, 4)):
                ar, ai = PWR[:, pi_, :], PWI[:, pi_, :]
                P.dve(I("tensor_tensor", tmpa[:], t_re(k), ar, ALU.mult), r=[("TK", k), "PW%dr" % pi_], w=["tmpa"])
                P.dve(I("tensor_tensor", tmpb[:], t_im(k), ai, ALU.mult), r=[("TK", k), "PW%di" % pi_], w=["tmpb"])
                P.dve(I("tensor_tensor", tmpa[:], tmpa[:], tmpb[:], ALU.subtract), r=["tmpa", "tmpb"], w=["tmpa"])
                P.dve(I("tensor_tensor", HIN[:, 0:32], HIN[:, 0:32], tmpa[:], ALU.add), r=["tmpa", "HIN"], w=["HIN"])
                P.dve(I("tensor_tensor", tmpa[:], t_re(k), ai, ALU.mult), r=[("TK", k), "HIN"], w=["tmpa"])
                P.dve(I("tensor_tensor", tmpb[:], t_im(k), ar, ALU.mult), r=[("TK", k), "HIN"], w=["tmpb"])
                P.dve(I("tensor_tensor", tmpa[:], tmpa[:], tmpb[:], ALU.add), r=["tmpa", "tmpb"], w=["tmpa"])
                P.dve(I("tensor_tensor", HIN[:, 32:64], HIN[:, 32:64], tmpa[:], ALU.add), r=["tmpa", "HIN"], w=["HIN"])
            P.dve(I("tensor_copy", HKc[:, 0, :, 0], HIN[:, 0:32]), r=["HIN"], w=["HK"])
            P.dve(I("tensor_copy", HKc[:, 0, :, 1], HIN[:, 32:64]), r=["HIN"], w=["HK"])
            for kk in range(1, 4):
                ar, ai = PWR[:, kk - 1, :], PWI[:, kk - 1, :]
                P.dve(I("tensor_tensor", tmpa[:], HIN[:, 0:32], ar, ALU.mult), r=["HIN", "HK"], w=["tmpa"])
                P.dve(I("tensor_tensor", tmpb[:], HIN[:, 32:64], ai, ALU.mult), r=["HIN", "HK"], w=["tmpb"])
                P.dve(I("tensor_tensor", HKc[:, kk, :, 0], tmpa[:], tmpb[:], ALU.subtract), r=["tmpa", "tmpb"], w=["HK"])
                P.dve(I("tensor_tensor", tmpa[:], HIN[:, 0:32], ai, ALU.mult), r=["HIN", "HK"], w=["tmpa"])
                P.dve(I("tensor_tensor", tmpb[:], HIN[:, 32:64], ar, ALU.mult), r=["HIN", "HK"], w=["tmpb"])
                P.dve(I("tensor_tensor", HKc[:, kk, :, 1], tmpa[:], tmpb[:], ALU.add), r=["tmpa", "tmpb"], w=["HK"])

        for ci in (1, 0):
            P.barrier()
            build_masks()
            P.barrier()
            attention_half(ci, 0)
            attention_half(ci, 1)
            P.barrier()
            if ci == 1:
                build_bc_carry()
                P.barrier()
                carry_scan()
                P.barrier()
            build_bc()
            P.barrier()
            ssm_project(ci)
            ssm_chunk(ci, correction=False)
            P.barrier()
            ssm_post(ci)
            P.barrier()
            out_proj(ci)
        P.emit()
    return nc


_NC_CACHE = {}


def _host_inputs(c, inp):
    f32 = np.float32
    bf = ml_dtypes.bfloat16
    q, sq = c % 4, c // 4
    xc = np.zeros((2, NBUF, 1024), f32)
    xc[0, OWN0:OWN0 + NOWN] = inp["x_prompt"][c]
    g0 = 2048 * q - 256
    lo, hi = max(g0, 0), min(g0 + NBUF, 8192)
    xc[1, lo - g0:hi - g0] = inp["x_sample"][sq, lo:hi]
    rep = lambda v, n=128: np.ascontiguousarray(np.broadcast_to(np.asarray(v, f32).reshape(1, -1), (n, np.asarray(v).size)))
    colm = lambda v: np.ascontiguousarray(np.asarray(v, f32).reshape(4, 128).T)
    def pl(a):
        a = np.asarray(a, f32).reshape(2, 16, 2, 64)
        return np.ascontiguousarray(a.transpose(2, 3, 0, 1).reshape(128, 32))
    def plb(a):
        a = np.asarray(a, f32).reshape(2, 16, 2, 64, 16)
        return np.ascontiguousarray(a.transpose(2, 3, 0, 1, 4).reshape(128, 512))
    def plc(a):
        a = np.asarray(a, f32).reshape(2, 16, 2, 16, 64)
        return np.ascontiguousarray(a.transpose(2, 4, 0, 1, 3).reshape(128, 512))
    logdt = np.broadcast_to(np.asarray(inp["log_dt"][0], f32)[:, :, None], (2, 32, 64))
    rpbp = np.zeros((8, 16, 127), f32)
    rpbp[:, :15, 48:79] = inp["rpb"][0]
    kc = np.arange(128) % 64
    qc = np.arange(64)
    qs = np.clip(qc - 8, 0, 48)
    cv = ((kc[:, None] >= qs[None, :]) & (kc[:, None] < qs[None, :] + 16)).astype(f32)
    a_ = (np.arange(128) // 64)
    rvi = np.zeros((128, 5, 2), f32)
    for oi, o in enumerate((-4, -2, 0, 2, 4)):
        for b in range(2):
            ro = o + a_ - b
            rvi[:, oi, b] = ((ro >= -4) & (ro <= 3))
    rvc = np.zeros((128, 2, 4, 7, 2), f32)
    for ci in range(2):
        R0, rows = (0, 32) if ci == 0 else (32 * q, 128)
        for jpb, jp in enumerate((0, 1, 14, 15)):
            for oi in range(7):
                o = 2 * oi - 6
                for b in range(2):
                    j = 2 * jp + b
                    ls = np.clip(j + R0 - 4, 0, rows - 8) - R0
                    kr = 2 * jp + o + a_
                    ok = (kr >= ls) & (kr < ls + 8) & (kr + R0 >= 0) & (kr + R0 < rows)
                    rvc[:, ci, jpb, oi, b] = ok
    kex = np.zeros((128, 2, 20), f32)
    kex[:, 0, 2:18] = 1.0
    for t in range(20):
        g = g0 + 128 * t
        kex[:, 1, t] = 1.0 if (0 <= g < 8192) else 0.0
    wsel = np.zeros((128, 3, 8, 64), f32)
    for r in range(8):
        if r // 4 != sq:
            continue
        qp = r % 4
        for col in range(64):
            d = (col % 32) // 16
            if d == 0 and qp < q:
                wsel[:, q - 1 - qp, r, col] = 1.0
            if d == 1 and qp > q:
                wsel[:, qp - q - 1, r, col] = 1.0
    iota1 = np.ascontiguousarray(np.broadcast_to(np.arange(1, 513, dtype=f32)[None, :], (128, 512)))
    slots = [(qq, 0) for qq in range(q)] + [(qq, 1) for qq in range(3, q, -1)]
    parts = []
    for qq, dd in slots:
        xs = inp["x_sample"][sq, 2048 * qq:2048 * qq + 2048]
        parts.append(xs[::-1] if dd == 1 else xs)
    xoth = np.ascontiguousarray(np.concatenate(parts, 0), dtype=f32)
    lsel = np.zeros((128, 9), f32)
    for si in range(3):
        dd = slots[si][1]
        if si > 0 and slots[si - 1][1] == dd:
            lsel[:, si] = 1.0
        last_of_chain = (si == 2) or (slots[si + 1][1] != dd)
        if last_of_chain:
            lsel[:, 3 + dd * 3 + si] = 1.0
    def widen(a32):
        return np.ascontiguousarray(np.concatenate([a32] + [a32[:, dd * 16:(dd + 1) * 16] for _, dd in slots], 1))
    def bsel(a512):
        a3 = a512.reshape(128, 32, 16)
        return np.ascontiguousarray(np.concatenate([a3[:, dd * 16:(dd + 1) * 16] for _, dd in slots], 1).reshape(128, 768))
    return {
        "xc": xc, "w_in": np.ascontiguousarray(inp["w_in"][0], f32), "w_out": np.ascontiguousarray(inp["w_out"][0], f32),
        "w_glu": np.ascontiguousarray(inp["w_glu"][0], f32),
        "ng": rep(inp["norm_g"][0]), "fg": rep(inp["final_norm_g"]), "ag": rep(inp["attn_out_g"][0]),
        "sgc": colm(inp["ssm_out_g"][0]), "bgc": colm(inp["b_glu"][0]), "dskc": colm(inp["d_skip"][0]),
        "lamr": widen(pl(inp["lam_re"][0])), "lami": widen(pl(inp["lam_im"][0])), "logdt": widen(pl(logdt)),
        "brec": bsel(plb(inp["b_re"][0])), "bimc": bsel(plb(inp["b_im"][0])), "lsel": lsel,
        "bre": plb(inp["b_re"][0]), "bim": plb(inp["b_im"][0]), "cre": plc(inp["c_re"][0]), "cim": plc(inp["c_im"][0]),
        "rpbp": rpbp, "cv": cv.astype(bf), "rvi": rvi.reshape(128, 10).astype(bf), "rvc": rvc.reshape(128, 112).astype(bf),
        "xoth": xoth, "kex": kex.reshape(128, 40), "iota1": iota1,
    }


def kernel(**inputs):
    inp = {k: np.asarray(v) for k, v in inputs.items()}
    if "nc" not in _NC_CACHE:
        _NC_CACHE["nc"] = build_program()
    nc = _NC_CACHE["nc"]
    in_maps = [_host_inputs(c, inp) for c in range(8)]
    res = run_bass_kernel_spmd(nc, in_maps, core_ids=list(range(8)))
    y_prompt = np.zeros((8, 2048, 1024), np.float32)
    y_sample = np.zeros((2, 8192, 1024), np.float32)
    for c in range(8):
        yo = np.asarray(res.results[c]["yo"], dtype=np.float32)
        y_prompt[c] = yo[0]
        y_sample[c // 4, 2048 * (c % 4):2048 * (c % 4) + 2048] = yo[1]
    return (y_prompt, y_sample)
```
